# Optimizing a Trainium2 kernel written in Bass

```python
import jax, jax.numpy as jnp
from jax import lax
import numpy as np

D_MODEL = 2048
BATCH = 8
SEQ = 2048
DEPTH = 1

D_MIX = D_MODEL
ROPE_THETA = 10000.0
RMS_EPS = 1e-6
NEG_INF = -1e30
TINY = 1e-30
Q_BLOCK = 128

WIDTH_A = D_MIX // 2
HEAD_DIM_A = 128
N_HEADS_A = WIDTH_A // HEAD_DIM_A
N_KV_A = 2
GROUP_A = N_HEADS_A // N_KV_A
KV_A = N_KV_A * HEAD_DIM_A
CMP_BLOCK = 32
CMP_STRIDE = 16
CMP_HIDDEN = 256
SEL_BLOCK = 64
SEL_TOPK = 16
SEL_Q_CHUNK = 32
WIN_A = 512
FORCE_SCORE = 1e4

WIDTH_B = D_MIX - WIDTH_A
HEAD_DIM_B = 64
N_HEADS_B = WIDTH_B // HEAD_DIM_B
N_KV_B = 2
GROUP_B = N_HEADS_B // N_KV_B
KV_B = N_KV_B * HEAD_DIM_B
WIN_B = 128

IN_SIZES = (WIDTH_A, KV_A, KV_A, KV_A, KV_A, KV_A, KV_A, WIDTH_A, 3 * N_HEADS_A,
            WIDTH_B, KV_B, KV_B, WIDTH_B)
D_IN = 2 * WIDTH_A + 6 * KV_A + 3 * N_HEADS_A + 2 * WIDTH_B + 2 * KV_B

kernel_name = 'nsa_swa_sink_hybrid_block'


def _rmsnorm(x, g):
    xf = x.astype(jnp.float32)
    y = xf * lax.rsqrt(jnp.mean(xf * xf, axis=-1, keepdims=True) + RMS_EPS)
    return (y * g.astype(jnp.float32)).astype(x.dtype)


def _heads(t, n, d):
    return t.reshape(t.shape[0], t.shape[1], n, d)


def _rope(x):
    S, d = x.shape[1], x.shape[-1]
    inv = ROPE_THETA ** (-jnp.arange(0, d, 2, dtype=jnp.float32) / d)
    ang = jnp.arange(S, dtype=jnp.float32)[:, None] * inv[None, :]
    cos = jnp.cos(ang)[None, :, None, :]
    sin = jnp.sin(ang)[None, :, None, :]
    xf = x.astype(jnp.float32)
    x1, x2 = xf[..., : d // 2], xf[..., d // 2:]
    return jnp.concatenate([x1 * cos - x2 * sin, x2 * cos + x1 * sin], axis=-1).astype(x.dtype)


def _masked_softmax(s, mask, sink=None):
    s = jnp.where(mask, s.astype(jnp.float32), NEG_INF)
    m = jnp.max(s, axis=-1, keepdims=True)
    if sink is not None:
        m = jnp.maximum(m, sink)
    e = jnp.where(mask, jnp.exp(s - m), 0.0)
    denom = jnp.sum(e, axis=-1, keepdims=True)
    if sink is not None:
        denom = denom + jnp.exp(sink - m)
    return e / jnp.maximum(denom, TINY)


def _compress(kv, pos_emb, w1, w2):
    B, S, Hkv, D = kv.shape
    n_c = (S - CMP_BLOCK) // CMP_STRIDE + 1
    idx = jnp.arange(n_c)[:, None] * CMP_STRIDE + jnp.arange(CMP_BLOCK)[None, :]
    blocks = kv[:, idx] + pos_emb[None, None, :, None, :].astype(kv.dtype)
    flat = blocks.transpose(0, 1, 3, 2, 4).reshape(B, n_c, Hkv, CMP_BLOCK * D)
    return jax.nn.silu(flat @ w1) @ w2


def _compressed_attention(q, kc, vc):
    S, D = q.shape[1], q.shape[-1]
    n_c = kc.shape[1]
    s = jnp.einsum('bskgd,bckd->bkgsc', q, kc) * (D ** -0.5)
    ends = jnp.arange(n_c) * CMP_STRIDE + CMP_BLOCK - 1
    mask = ends[None, :] <= jnp.arange(S)[:, None]
    p = _masked_softmax(s, mask)
    o = jnp.einsum('bkgsc,bckd->bskgd', p.astype(vc.dtype), vc)
    return o, p


def _select_blocks(p_cmp, S):
    n_c = p_cmp.shape[-1]
    n_sel = S // SEL_BLOCK
    c_start = np.arange(n_c) * CMP_STRIDE
    j_start = np.arange(n_sel) * SEL_BLOCK
    overlap = (c_start[:, None] < j_start[None, :] + SEL_BLOCK) & (c_start[:, None] + CMP_BLOCK > j_start[None, :])
    p_sel = jnp.einsum('bkgsc,cj->bksj', p_cmp, jnp.asarray(overlap, jnp.float32))
    t = jnp.arange(S)[:, None]
    j = jnp.arange(n_sel)[None, :]
    cur = t // SEL_BLOCK
    forced = (j == 0) | (j == cur) | (j == cur - 1)
    valid = j * SEL_BLOCK <= t
    score = jnp.where(forced, FORCE_SCORE, jnp.where(valid, p_sel, -1.0))
    _, idx = lax.top_k(score, min(SEL_TOPK, n_sel))
    return idx


def _selected_attention(q, k, v, idx):
    B, S, Hkv, G, D = q.shape
    K = idx.shape[-1]
    n_sel = S // SEL_BLOCK
    n_ch = S // SEL_Q_CHUNK
    kb = k.reshape(B, n_sel, SEL_BLOCK, Hkv, D).transpose(0, 3, 1, 2, 4)
    vb = v.reshape(B, n_sel, SEL_BLOCK, Hkv, D).transpose(0, 3, 1, 2, 4)
    gather = jax.vmap(jax.vmap(lambda blocks, i: blocks[i]))
    scale = D ** -0.5

    def chunk_fn(args):
        qc, ic, tc = args
        kg = gather(kb, ic)
        vg = gather(vb, ic)
        s = jnp.einsum('bckgd,bkcnld->bkgcnl', qc, kg) * scale
        tok = ic[..., None] * SEL_BLOCK + jnp.arange(SEL_BLOCK)
        mask = (tok <= tc[None, None, :, None, None])[:, :, None]
        C = qc.shape[1]
        p = _masked_softmax(s.reshape(B, Hkv, G, C, K * SEL_BLOCK), mask.reshape(B, Hkv, 1, C, K * SEL_BLOCK))
        p = p.reshape(B, Hkv, G, C, K, SEL_BLOCK)
        return jnp.einsum('bkgcnl,bkcnld->bckgd', p.astype(vg.dtype), vg)

    qs = jnp.moveaxis(q.reshape(B, n_ch, SEL_Q_CHUNK, Hkv, G, D), 1, 0)
    ids = jnp.moveaxis(idx.reshape(B, Hkv, n_ch, SEL_Q_CHUNK, K), 2, 0)
    ts = jnp.arange(S).reshape(n_ch, SEL_Q_CHUNK)
    o = lax.map(chunk_fn, (qs, ids, ts))
    return jnp.moveaxis(o, 0, 1).reshape(B, S, Hkv, G, D)


def _banded_attention(q, k, v, window, sink=None):
    B, S, Hkv, G, D = q.shape
    nq = S // Q_BLOCK
    kv_len = window + Q_BLOCK
    pad = ((0, 0), (window, 0), (0, 0), (0, 0))
    idx = jnp.arange(nq)[:, None] * Q_BLOCK + jnp.arange(kv_len)[None, :]
    kb = jnp.pad(k, pad)[:, idx]
    vb = jnp.pad(v, pad)[:, idx]
    qb = q.reshape(B, nq, Q_BLOCK, Hkv, G, D)
    s = jnp.einsum('bnqkgd,bnjkd->bnkgqj', qb, kb) * (D ** -0.5)
    key_pos = idx - window
    q_pos = jnp.arange(S).reshape(nq, Q_BLOCK)
    diff = q_pos[:, :, None] - key_pos[:, None, :]
    mask = ((diff >= 0) & (diff < window) & (key_pos[:, None, :] >= 0))[None, :, None, None]
    sink_b = None if sink is None else sink.astype(jnp.float32)[None, None, :, :, None, None]
    p = _masked_softmax(s, mask, sink_b)
    o = jnp.einsum('bnkgqj,bnjkd->bnqkgd', p.astype(vb.dtype), vb)
    return o.reshape(B, S, Hkv, G, D)


def setup_inputs(seed: int = 0) -> dict:
    key = jax.random.key(seed)
    ks = jax.random.split(key, 12)

    def nrm(k, shape, scale):
        return jax.random.normal(k, shape, jnp.float32) * scale

    flat_in = CMP_BLOCK * HEAD_DIM_A
    return {
        'x': nrm(ks[0], (BATCH, SEQ, D_MODEL), 1.0),
        'w_in': nrm(ks[1], (DEPTH, D_MODEL, D_IN), D_MODEL ** -0.5),
        'cmp_k_w1': nrm(ks[2], (DEPTH, flat_in, CMP_HIDDEN), flat_in ** -0.5),
        'cmp_k_w2': nrm(ks[3], (DEPTH, CMP_HIDDEN, HEAD_DIM_A), CMP_HIDDEN ** -0.5),
        'cmp_v_w1': nrm(ks[4], (DEPTH, flat_in, CMP_HIDDEN), flat_in ** -0.5),
        'cmp_v_w2': nrm(ks[5], (DEPTH, CMP_HIDDEN, HEAD_DIM_A), CMP_HIDDEN ** -0.5),
        'cmp_k_pos': nrm(ks[6], (DEPTH, CMP_BLOCK, HEAD_DIM_A), 0.1),
        'cmp_v_pos': nrm(ks[7], (DEPTH, CMP_BLOCK, HEAD_DIM_A), 0.1),
        'sinks': nrm(ks[8], (DEPTH, N_HEADS_B), 1.0),
        'w_out': nrm(ks[9], (DEPTH, D_MIX, D_MODEL), D_MIX ** -0.5),
        'norm_g': 1.0 + nrm(ks[10], (DEPTH, D_MODEL), 0.01),
        'final_g': 1.0 + nrm(ks[11], (D_MODEL,), 0.01),
    }


def reference(x, w_in, cmp_k_w1, cmp_k_w2, cmp_v_w1, cmp_v_w2, cmp_k_pos, cmp_v_pos, sinks, w_out, norm_g, final_g):
    B, S, _ = x.shape
    offsets = np.cumsum(IN_SIZES)[:-1].tolist()
    for l in range(DEPTH):
        h = _rmsnorm(x, norm_g[l])
        proj = h @ w_in[l]
        (qa, kca, vca, ksa, vsa, kwa, vwa, za, ga,
         qb, kb, vb, zb) = jnp.split(proj, offsets, axis=-1)

        qa = _rope(_heads(qa, N_HEADS_A, HEAD_DIM_A)).reshape(B, S, N_KV_A, GROUP_A, HEAD_DIM_A)
        kc = _compress(_rope(_heads(kca, N_KV_A, HEAD_DIM_A)), cmp_k_pos[l], cmp_k_w1[l], cmp_k_w2[l])
        vc = _compress(_heads(vca, N_KV_A, HEAD_DIM_A), cmp_v_pos[l], cmp_v_w1[l], cmp_v_w2[l])
        o_cmp, p_cmp = _compressed_attention(qa, kc, vc)
        sel_idx = _select_blocks(p_cmp, S)
        o_sel = _selected_attention(qa, _rope(_heads(ksa, N_KV_A, HEAD_DIM_A)), _heads(vsa, N_KV_A, HEAD_DIM_A), sel_idx)
        o_win = _banded_attention(qa, _rope(_heads(kwa, N_KV_A, HEAD_DIM_A)), _heads(vwa, N_KV_A, HEAD_DIM_A), WIN_A)
        gates = jax.nn.sigmoid(ga.astype(jnp.float32)).reshape(B, S, N_KV_A, GROUP_A, 3).astype(x.dtype)
        o_a = gates[..., 0:1] * o_cmp + gates[..., 1:2] * o_sel + gates[..., 2:3] * o_win
        y_a = o_a.reshape(B, S, WIDTH_A) * jax.nn.silu(za)

        qb = _rope(_heads(qb, N_HEADS_B, HEAD_DIM_B)).reshape(B, S, N_KV_B, GROUP_B, HEAD_DIM_B)
        o_b = _banded_attention(qb, _rope(_heads(kb, N_KV_B, HEAD_DIM_B)), _heads(vb, N_KV_B, HEAD_DIM_B),
                                WIN_B, sink=sinks[l].reshape(N_KV_B, GROUP_B))
        y_b = o_b.reshape(B, S, WIDTH_B) * jax.nn.silu(zb)

        x = x + jnp.concatenate([y_a, y_b], axis=-1) @ w_out[l]
    return _rmsnorm(x, final_g)
```

```python
import math
from contextlib import ExitStack

import numpy as np
import ml_dtypes

import concourse.bass as bass
import concourse.mybir as mybir
from concourse.bass_utils import run_bass_kernel_spmd

F32 = mybir.dt.float32
BF16 = mybir.dt.bfloat16
ALU = mybir.AluOpType
ACTF = mybir.ActivationFunctionType
AX = mybir.AxisListType

S = 2048
D = 2048
NT = 16
NQG = 4
RMS_EPS = 1e-6
TINY = 1e-30
NEGBIG = -30000.0
B_PE_BIAS = False


class DmaSem:
    def __init__(self, sem):
        self.sem = sem
        self.n = 0


class Ev:
    __slots__ = ("holder", "val")

    def __init__(self, holder, val):
        self.holder = holder
        self.val = val


class Eng:
    def __init__(self, nc, ctx, eng, name):
        self.nc = nc
        self.e = eng
        self.name = name
        self.sem = ctx.enter_context(nc.semaphore("sem_" + name))
        self.n = 0
        self.waited = {}

    def wait(self, ev):
        if ev is None:
            return
        h = ev.holder
        val = h.n if isinstance(h, DmaSem) else ev.val
        key = id(h)
        if self.waited.get(key, 0) >= val:
            return
        self.waited[key] = val
        self.e.wait_ge(h.sem, val)

    def done(self, inst):
        self.n += 1
        inst.then_inc(self.sem, 1)
        return Ev(self, self.n)


class T:
    def __init__(self, name="", excl=False):
        self.name = name
        self.w = None
        self.r = []
        self.excl = excl


def _split(reads, writes):
    ex = [t for t in reads if t.excl]
    if ex:
        reads = [t for t in reads if not t.excl]
        writes = list(writes) + [t for t in ex if t not in writes]
    return reads, writes


def _pre(E, reads, writes):
    for t in reads:
        E.wait(t.w)
    for t in writes:
        E.wait(t.w)
        for r in t.r:
            E.wait(r)


def _post(ev, reads, writes):
    for t in reads:
        t.r.append(ev)
    for t in writes:
        t.w = ev
        t.r = []


def op(E, fn, reads=(), writes=()):
    reads, writes = _split(reads, writes)
    _pre(E, reads, writes)
    inst = fn(E.e)
    ev = E.done(inst)
    _post(ev, reads, writes)
    return ev


def grp(E, fns, reads=(), writes=()):
    reads, writes = _split(reads, writes)
    _pre(E, reads, writes)
    inst = None
    for fn in fns:
        inst = fn(E.e)
    ev = E.done(inst)
    _post(ev, reads, writes)
    return ev


def dma(E, dsem, out, in_, reads=(), writes=(), **kw):
    _pre(E, reads, writes)
    E.e.dma_start(out=out, in_=in_, **kw).then_inc(dsem.sem, 16)
    dsem.n += 16
    ev = Ev(dsem, dsem.n)
    _post(ev, reads, writes)
    return ev


def bc_mid(ap2d, n):
    a = ap2d.ap
    return bass.AP(ap2d.tensor, ap2d.offset, [list(a[0]), [0, n]] + [list(x) for x in a[1:]])


def bc_last(ap2d, n):
    a = ap2d.ap
    return bass.AP(ap2d.tensor, ap2d.offset, [list(x) for x in a] + [[0, n]])


def _consts():
    bf = ml_dtypes.bfloat16
    c = {}
    pos = np.arange(S, dtype=np.float32)

    def tab(d):
        inv = (np.float32(10000.0) ** (-(np.arange(0, d, 2, dtype=np.float32)) / np.float32(d))).astype(np.float32)
        ang = (pos[:, None] * inv[None, :]).astype(np.float32)
        return np.cos(ang).astype(np.float32), np.sin(ang).astype(np.float32)

    cA, sA = tab(128)
    cB, sB = tab(64)
    rt = np.concatenate([cA, cA, -sA, sA, cB, cB, -sB, sB], axis=1).astype(np.float32)
    c["ropetab"] = np.ascontiguousarray(rt.reshape(NT, 128, 384))
    c["ident"] = np.eye(128, dtype=np.float32).astype(bf)
    p = np.arange(128)
    c["tri"] = (p[:, None] <= p[None, :]).astype(np.float32).astype(bf)
    c["atri"] = (p[None, :] < p[:, None]).astype(np.float32).astype(bf)
    vis = np.concatenate([(p[:, None] <= p[None, :]), (p[None, :] < p[:, None])], axis=1).astype(np.float32)
    c["ntrat"] = ((vis - 1.0) * 30000.0).astype(bf)
    cc = np.arange(128)
    cm = ((cc[:, None] * 16 + 31) <= np.arange(S)[None, :]).astype(np.float32)
    cm[127, :] = 0.0
    c["cmask"] = ((cm - 1.0) * 30000.0).astype(bf)
    cs = np.arange(127) * 16
    js = np.arange(32) * 64
    ov = ((cs[:, None] < js[None, :] + 64) & (cs[:, None] + 32 > js[None, :])).astype(np.float32)
    ovp = np.zeros((128, 32), np.float32)
    ovp[:127] = ov
    c["overlap"] = ovp.astype(bf)
    es = (np.arange(S)[None, :] // 64 == np.arange(32)[:, None]).astype(np.float32)
    esp = np.zeros((128, S), np.float32)
    esp[:32] = es
    c["esel"] = esp.astype(bf)
    t = np.arange(S)[:, None]
    j = np.arange(32)[None, :]
    cur = t // 64
    forced = (j == 0) | (j == cur) | (j == cur - 1)
    valid = (j * 64) <= t
    vmask = (valid & ~forced).astype(np.float32)
    cconst = np.where(forced, 1e4, np.where(valid, 0.0, -1.0)).astype(np.float32)
    sc = np.stack([vmask, cconst], axis=1)
    c["selc"] = np.ascontiguousarray(sc.reshape(NT, 128, 64))
    return c


def _split_w_in(w_in):
    sizes = (1024, 256, 256, 256, 256, 256, 256, 1024, 24, 1024, 128, 128, 1024)
    offs = np.cumsum((0,) + sizes)
    names = ["qa", "kca", "vca", "ksa", "vsa", "kwa", "vwa", "za", "ga", "qb", "kb", "vb", "zb"]
    cols = {n: w_in[:, offs[i]:offs[i + 1]] for i, n in enumerate(names)}
    w_kv = np.concatenate([cols["kca"], cols["ksa"], cols["kwa"], cols["kb"], cols["vb"],
                           cols["vca"], cols["vsa"], cols["vwa"]], axis=1)
    w_q1 = np.concatenate([cols["qa"], cols["ga"]], axis=1)
    w_q2 = np.concatenate([cols["za"], cols["qb"]], axis=1)
    w_z2 = cols["zb"]
    return (np.ascontiguousarray(w_kv), np.ascontiguousarray(w_q1),
            np.ascontiguousarray(w_q2), np.ascontiguousarray(w_z2))


def build(debug=None, stop_after=None):
    debug = debug or {}
    nc = bass.Bass("TRN2", target_bir_lowering=False)
    ctx = ExitStack()

    def din(name, shape, dt=F32):
        return nc.dram_tensor(name, list(shape), dt, kind="ExternalInput").ap()

    x_d = din("x", [S, D])
    wkv_d = din("w_kv", [D, 1792])
    wq1_d = din("w_q1", [D, 1048])
    wq2_d = din("w_q2", [D, 2048])
    wz2_d = din("w_z2", [D, 1024])
    wout_d = din("w_out", [D, D])
    ckw1_d = din("cmp_k_w1", [4096, 256])
    ckw2_d = din("cmp_k_w2", [256, 128])
    cvw1_d = din("cmp_v_w1", [4096, 256])
    cvw2_d = din("cmp_v_w2", [256, 128])
    kposT_d = din("kposT", [128, 32])
    vposT_d = din("vposT", [128, 32])
    sinks_d = din("sinks_bc", [128, 16])
    gcol_d = din("g_col", [128, 16])
    fg_d = din("fg_bc", [128, D])
    ropetab_d = din("ropetab", [NT, 128, 384])
    ident_d = din("ident", [128, 128], BF16)
    tri_d = din("tri", [128, 128], BF16)
    atri_d = din("atri", [128, 128], BF16)
    ntrat_d = din("ntrat", [128, 256], BF16)
    cmask_d = din("cmask", [128, S], BF16)
    overlap_d = din("overlap", [128, 32], BF16)
    esel_d = din("esel", [128, S], BF16)
    selc_d = din("selc", [NT, 128, 64])
    out_d = nc.dram_tensor("out", [S, D], F32, kind="ExternalOutput").ap()
    dbg_d = {}

    def dout(name, shape, dt=F32):
        dbg_d[name] = nc.dram_tensor(name, list(shape), dt, kind="ExternalOutput").ap()
        return dbg_d[name]

    PE = Eng(nc, ctx, nc.tensor, "pe")
    ACT = Eng(nc, ctx, nc.scalar, "act")
    DVE = Eng(nc, ctx, nc.vector, "dve")
    POOL = Eng(nc, ctx, nc.gpsimd, "pool")
    SP = Eng(nc, ctx, nc.sync, "sp")
    ENGS = [PE, ACT, DVE, POOL, SP]
    all_tiles = []
    all_dsems = []

    def TT(name, excl=False):
        t = T(name, excl)
        all_tiles.append(t)
        return t

    def dsem(name):
        d = DmaSem(ctx.enter_context(nc.semaphore("d_" + name)))
        all_dsems.append(d)
        return d

    def barrier():
        for E in ENGS:
            for F in ENGS:
                if F is not E and F.n > 0:
                    E.wait(Ev(F, F.n))
            for d in all_dsems:
                if d.n > 0:
                    E.wait(Ev(d, d.n))
        for t in all_tiles:
            t.w = None
            t.r = []

    def sb(name, shape, dt):
        return ctx.enter_context(nc.sbuf_tensor(name, list(shape), dt))

    def ps(name, shape, dt):
        return ctx.enter_context(nc.psum_tensor(name, list(shape), dt))

    def flat(ap3):
        return ap3.rearrange("p a b -> p (a b)")

    QY = sb("QY", [128, 16, S], BF16)
    R2 = sb("R2", [128, 16, S], BF16)
    RLO = flat(R2[:, 0:8, :]).rearrange("p (k c) -> p k c", k=16)
    RHI = flat(R2[:, 8:16, :]).rearrange("p (k c) -> p k c", k=16)
    KT_A = R2[:, 0:6, :]
    VcaT = R2[:, 6:8, :]
    VS = flat(R2[:, 8:11, :])[:, 0:4128].rearrange("p (t k d) -> p t k d", t=NT, k=2)
    VW = flat(R2[:, 11:14, :])[:, 0:4128].rearrange("p (t k d) -> p t k d", t=NT, k=2)
    KT_B = sb("KT_B", [128, 2, S], BF16)
    VB = sb("VB", [128, NT, 2, 65], BF16)
    XS = [sb(f"xs{i}", [128, D], F32) for i in range(2)]
    HB = sb("hb", [128, D], BF16)
    HT = [sb(f"hT{i}", [128, 16, 128], BF16) for i in range(2)]
    RT = [sb(f"rt{i}", [128, 384], F32) for i in range(2)]
    SCR = sb("scr", [128, 7168], BF16)
    TA = SCR[:, 0:1024].bitcast(F32)
    TB = SCR[:, 1024:2048].bitcast(F32)
    KR = [SCR[:, 2048 + i * 512:2560 + i * 512] for i in range(4)]
    PX = [SCR[:, i * 512:(i + 1) * 512] for i in range(4)] + [sb("px4", [128, 512], BF16)]
    ACC = [SCR[:, 2048 + g * 1024:3072 + g * 1024].bitcast(F32).rearrange("p (t d) -> p t d", t=4) for g in range(4)]
    OBF = SCR[:, 2048:2560].rearrange("p (t d) -> p t d", t=4)
    TMPA = SCR[:, 6144:7168].bitcast(F32).rearrange("p (t d) -> p t d", t=4)
    IDENT = sb("ident_s", [128, 128], BF16)
    TRI = sb("tri_s", [128, 128], BF16)
    ATRI = sb("atri_s", [128, 128], BF16)
    TRAT01 = sb("trat01_s", [128, 256], BF16)
    TRAT = TRAT01
    CMASK = sb("cmask_s", [128, S], BF16)
    ESEL = sb("esel_s", [128, S], BF16)
    SELB = sb("selb", [128, 2, 1024], BF16)
    GAW = sb("gaw", [128, 16, 24], BF16)
    GATES = sb("gates", [128, NT, 24], F32)
    KCT = sb("kct", [128, 2, 128], BF16)
    VCX = sb("vcx", [128, 2, 168], BF16)
    W2K = sb("w2k", [128, 2, 128], BF16)
    W2V = sb("w2v", [128, 2, 128], BF16)
    KPOS = sb("kpos", [128, 32], F32)
    VPOS = sb("vpos", [128, 32], F32)
    GCOL = sb("gcol", [128, 16], F32)
    RSTD = sb("rstd", [128, NT], F32)
    SS = sb("ss", [128, NT], F32)
    SS2 = sb("ss2", [128, NT], F32)
    RS2 = sb("rs2", [128, NT], F32)
    EPS_T = sb("eps_t", [128, 1], F32)
    ESINK = sb("esink", [128, 16], F32)
    SELCQ = sb("selcq", [128, 4, 64], F32)
    PSEL = sb("psel", [128, 4, 32], F32)
    SCO = sb("sco", [128, 4, 32], F32)
    SC2 = sb("sc2", [128, 32], F32)
    M8 = sb("m8", [128, 8], F32)
    M8b = sb("m8b", [128, 8], F32)
    NB = sb("nb", [128, 4, 32], BF16)
    RD = sb("rd", [128, 8], F32)
    RD2 = sb("rd2", [128, 8], F32)
    GW = sb("gw", [128, 4], F32)
    TMPS = sb("tmps", [128, 4, 32], F32)

    _b012 = [ps(f"bk{i}", [128, 512], F32) for i in range(3)]
    PVT = [ps(f"pvt{i}", [128, 1024], F32) for i in range(2)]
    _b7 = ps("bk7", [128, 512], F32)
    BK = [_b012[0][:], _b012[1][:], _b012[2][:], PVT[0][:, 0:512], PVT[0][:, 512:1024],
          PVT[1][:, 0:512], PVT[1][:, 512:1024], _b7[:]]
    BKb = [BK[i].bitcast(BF16) for i in range(8)]
    PM = BK[0:4]

    t_xs = [TT("xs0"), TT("xs1")]
    t_hb = TT("hb")
    t_ht = [TT("ht0"), TT("ht1")]
    t_rt = [TT("rt0"), TT("rt1")]
    t_ta, t_tb = TT("ta"), TT("tb")
    t_kr = [TT(f"kr{i}") for i in range(4)]
    t_bk = [TT(f"bk{i}", True) for i in range(8)]
    t_pm = t_bk[0:4]
    t_slabb = [TT(f"slab{i}") for i in range(5)]
    t_const = TT("const")
    t_rstd = TT("rstd")
    t_kv = TT("kv")
    t_q = [[TT(f"q{c}_{q}") for q in range(NQG)] for c in range(16)]
    t_gates = TT("gates")
    t_px = [TT(f"px{i}") for i in range(5)]
    t_acc = [TT(f"acc{g}") for g in range(4)]
    t_obf = TT("obf")
    t_rd = TT("rd")
    t_tmpa = TT("tmpa")
    t_sel = TT("sel")
    t_selb = TT("selb")
    t_selc = TT("selc")
    t_cmpw = TT("cmpw")
    t_blk = [TT(f"blk{i}") for i in range(4)]
    t_hid = [TT(f"hid{i}") for i in range(8)]
    t_kc = TT("kc")

    d_xs = [dsem("xs0"), dsem("xs1")]
    d_rt = [dsem("rt0"), dsem("rt1")]
    d_const = dsem("const")
    d_slab = dsem("slab")
    d_slabb = [dsem(f"slabb{i}") for i in range(5)]
    d_out = [dsem("out0"), dsem("out1")]
    d_selc = dsem("selc")

    t_const2 = TT("const2")
    d_const2 = dsem("const2")
    for dst, src in ((IDENT, ident_d), (GCOL, gcol_d)):
        dma(SP, d_const, dst[:], src, writes=[t_const])

    def late_consts():
        dma(SP, d_const2, TRAT01[:, 0:128], tri_d, writes=[t_const2])
        dma(SP, d_const2, TRAT01[:, 128:256], atri_d, writes=[t_const2])
        for dst, src in ((TRI, tri_d), (ATRI, atri_d), (CMASK, cmask_d), (ESEL, esel_d),
                         (KPOS, kposT_d), (VPOS, vposT_d), (ESINK, sinks_d)):
            dma(SP, d_const2, dst[:], src, writes=[t_const2])

    op(DVE, lambda e: e.memset(EPS_T[:], RMS_EPS), writes=[t_const])
    op(DVE, lambda e: e.memset(SS[:], 0.0), writes=[t_rstd])
    op(DVE, lambda e: e.memset(SS2[:], 0.0), writes=[t_rstd])
    op(DVE, lambda e: e.memset(SELB[:], 0.0), writes=[t_selb])
    op(DVE, lambda e: e.memset(VS[:, :, :, 128:129], 1.0), writes=[t_kv])
    op(DVE, lambda e: e.memset(VW[:, :, :, 128:129], 1.0), writes=[t_kv])
    op(DVE, lambda e: e.memset(VB[:, :, :, 64:65], 1.0), writes=[t_kv])

    def load_slab(dst, w_d, c0, ncols, blk0=0, dcol0=0, cb=512):
        wv = w_d.rearrange("(kc p) c -> p kc c", p=128)
        for j, cc in enumerate(range(0, ncols, cb)):
            w = min(cb, ncols - cc)
            for kc in range(0, 16, 8):
                dma(POOL, d_slabb[blk0 + j], dst[:, kc:kc + 8, dcol0 + cc:dcol0 + cc + w],
                    wv[:, kc:kc + 8, c0 + cc:c0 + cc + w], writes=[t_slabb[blk0 + j]])

    xt_count = [0]
    pm_count = [0]
    pending = []

    def flush_pending():
        while pending:
            pending.pop(0)()

    def proj_pass(blocks, post, first=False, need_rt=False):
        xv = x_d.rearrange("(t p) d -> t p d", p=128)
        bufs = {}

        def issue_x(tt):
            b = xt_count[0] % 2
            xt_count[0] += 1
            dma(SP, d_xs[b], XS[b][:], xv[tt], writes=[t_xs[b]])
            bufs[tt] = b

        def issue_rt(tt):
            if need_rt:
                b = bufs[tt]
                dma(SP, d_rt[b], RT[b][:], ropetab_d[tt], writes=[t_rt[b]])

        def prep_act(tt):
            b = bufs[tt]
            if first:
                op(ACT, lambda e: e.activation(out=HB[:], in_=XS[b][:], func=ACTF.Square,
                                               accum_out=SS[:, tt:tt + 1]),
                   reads=[t_xs[b]], writes=[t_hb, t_rstd])
                op(ACT, lambda e: e.activation(out=SS[:, tt:tt + 1], in_=SS[:, tt:tt + 1], func=ACTF.Sqrt,
                                               scale=1.0 / D, bias=EPS_T[:, 0:1]),
                   reads=[t_const], writes=[t_rstd])
                op(DVE, lambda e: e.reciprocal(out=RSTD[:, tt:tt + 1], in_=SS[:, tt:tt + 1]),
                   writes=[t_rstd])
            op(ACT, lambda e: e.mul(out=HB[:], in_=XS[b][:], mul=RSTD[:, tt:tt + 1]),
               reads=[t_xs[b], t_rstd], writes=[t_hb])
            if tt + 2 < NT:
                issue_x(tt + 2)

        def prep_pe(tt):
            hb_i = tt % 2
            for half in range(2):
                bk = 4 + half
                grp(PE, [(lambda e, kc=kc: e.transpose(out=BKb[bk][:, (kc % 8) * 128:(kc % 8 + 1) * 128],
                                                        in_=HB[:, kc * 128:(kc + 1) * 128],
                                                        identity=IDENT[:]))
                         for kc in range(half * 8, half * 8 + 8)],
                    reads=[t_hb, t_const], writes=[t_bk[bk]])
                if half == 0 and not first:
                    grp(ACT, [(lambda e, kc=kc: e.mul(out=HT[hb_i][:, kc, :], in_=BKb[bk][:, kc * 128:(kc + 1) * 128],
                                                      mul=GCOL[:, kc:kc + 1])) for kc in range(8)],
                        reads=[t_bk[bk], t_const], writes=[t_ht[hb_i]])
                else:
                    op(DVE, lambda e: e.tensor_tensor(out=HT[hb_i][:, half * 8:half * 8 + 8, :],
                                                      in0=BKb[bk].rearrange("p (k t) -> p k t", k=8),
                                                      in1=bc_last(GCOL[:, half * 8:half * 8 + 8], 128), op=ALU.mult),
                       reads=[t_bk[bk], t_const], writes=[t_ht[hb_i]])

        issue_x(0)
        issue_rt(0)
        issue_x(1)
        issue_rt(1)
        if first:
            late_consts()
        prep_act(0)
        prep_pe(0)
        nb = len(blocks)
        for tt in range(NT):
            b = bufs[tt]
            hb_i = tt % 2
            if tt + 1 < NT:
                prep_act(tt + 1)
            for bi, (slab, c0, w, sti) in enumerate(blocks):
                pi = pm_count[0] % 4
                pm_count[0] += 1
                pm = PM[pi]
                grp(PE, [(lambda e, kc=kc: e.matmul(pm[:, 0:w], lhsT=HT[hb_i][:, kc, :],
                                                    rhs=slab[:, kc, c0:c0 + w],
                                                    start=(kc == 0), stop=(kc == 15)))
                         for kc in range(16)],
                    reads=[t_ht[hb_i], t_slabb[sti]], writes=[t_pm[pi]])
                if w >= 256:
                    flush_pending()
                post(tt, bi, pm, t_pm[pi], w, b)
                if bi == max(0, nb - 3) and tt + 1 < NT:
                    prep_pe(tt + 1)
            if tt + 2 < NT:
                issue_rt(tt + 2)
        flush_pending()

    def rope(pm, t_pmi, c0, nh, d, rt_b, outs):
        hd = d // 2
        base = 0 if d == 128 else 256
        t_rtb = t_rt[rt_b]
        cos2 = RT[rt_b][:, base:base + d]
        nsin = RT[rt_b][:, base + d:base + d + hd]
        psin = RT[rt_b][:, base + d + hd:base + 2 * d]
        w = nh * d
        psv = pm[:, c0:c0 + w].rearrange("p (h d) -> p h d", h=nh)
        tav = TA[:, 0:w].rearrange("p (h d) -> p h d", h=nh)
        tbv = TB[:, 0:w].rearrange("p (h d) -> p h d", h=nh)
        op(DVE, lambda e: e.tensor_tensor(out=tav, in0=psv, in1=bc_mid(cos2, nh), op=ALU.mult),
           reads=[t_pmi, t_rtb], writes=[t_ta])
        op(DVE, lambda e: e.tensor_tensor(out=tbv[:, :, 0:hd], in0=psv[:, :, hd:d], in1=bc_mid(nsin, nh), op=ALU.mult),
           reads=[t_pmi, t_rtb], writes=[t_tb])
        op(DVE, lambda e: e.tensor_tensor(out=tbv[:, :, hd:d], in0=psv[:, :, 0:hd], in1=bc_mid(psin, nh), op=ALU.mult),
           reads=[t_pmi, t_rtb], writes=[t_tb])
        for (o, t_o) in outs:
            op(DVE, lambda e, o=o: e.tensor_tensor(out=o, in0=tav, in1=tbv, op=ALU.add),
               reads=[t_ta, t_tb], writes=[t_o])

    kr_cnt = [0, 0]

    def transposes(src_tile, t_src, nchunks, dsts, eng=None):
        def run():
            bk = 6 + kr_cnt[1] % 2
            kr_cnt[1] += 1
            grp(PE, [(lambda e, c=c: e.transpose(out=BKb[bk][:, c * 128:(c + 1) * 128],
                                                  in_=src_tile[:, c * 128:(c + 1) * 128], identity=IDENT[:]))
                     for c in range(nchunks)],
                reads=[t_src, t_const], writes=[t_bk[bk]])
            src = BKb[bk][:, 0:nchunks * 128].rearrange("p (k t) -> p k t", k=nchunks)
            for (dst_ap, lo, hi, tl) in dsts:
                op(ACT, lambda e, dst_ap=dst_ap, lo=lo, hi=hi: e.copy(out=dst_ap, in_=src[:, lo:hi, :]),
                   reads=[t_bk[bk]], writes=tl)
        kr_cnt[0] += 1
        pending.append(run)

    load_slab(QY, wkv_d, 0, 1792)
    t_w1k = TT("w1k")
    d_w1k = dsem("w1k")
    dma(POOL, d_w1k, W2K[:], ckw2_d.rearrange("(c j) d -> j c d", j=128), writes=[t_w1k])
    dma(POOL, d_w1k, W2V[:], cvw2_d.rearrange("(c j) d -> j c d", j=128), writes=[t_w1k])
    wvk = ckw1_d.rearrange("(l d) j -> d l j", d=128)
    W1KA = flat(R2[:, 14:16, :]).rearrange("p (l j) -> p l j", l=16)
    W1KB = SCR[:, 4096:7168].rearrange("p (l j) -> p l j", l=12)
    W1KC = flat(R2[:, 8:11, :])[:, 4128:5152].rearrange("p (l j) -> p l j", l=4)
    dma(POOL, d_w1k, W1KA[:, 0:8, :], wvk[:, 0:8, :], writes=[t_w1k])
    dma(POOL, d_w1k, W1KA[:, 8:16, :], wvk[:, 8:16, :], writes=[t_w1k])
    dma(POOL, d_w1k, W1KB[:, 0:8, :], wvk[:, 16:24, :], writes=[t_w1k])
    dma(POOL, d_w1k, W1KB[:, 8:12, :], wvk[:, 24:28, :], writes=[t_w1k])
    dma(POOL, d_w1k, W1KC, wvk[:, 28:32, :], writes=[t_w1k])
    W1K_L = [W1KA[:, l, :] for l in range(16)] + [W1KB[:, l, :] for l in range(12)] + [W1KC[:, l, :] for l in range(4)]

    def post1(tt, blk, pm, t_pmi, w, b):
        ts_ = slice(tt * 128, (tt + 1) * 128)
        ki = kr_cnt[0] % 4
        kr = KR[ki]
        if blk == 0:
            rope(pm, t_pmi, 0, 4, 128, b, [(kr[:, 0:512].rearrange("p (h d) -> p h d", h=4), t_kr[ki])])
            transposes(kr, t_kr[ki], 4, [(KT_A[:, 0:4, ts_], 0, 4, [t_kv])])
        elif blk == 1:
            rope(pm, t_pmi, 0, 2, 128, b, [(kr[:, 0:256].rearrange("p (h d) -> p h d", h=2), t_kr[ki])])
            kdup = kr[:, 256:512].rearrange("p (k c d) -> p k c d", k=2, c=2)
            rope(pm, t_pmi, 256, 2, 64, b, [(kdup[:, :, 0, :], t_kr[ki]), (kdup[:, :, 1, :], t_kr[ki])])
            op(ACT, lambda e: e.copy(out=VB[:, tt, :, 0:64], in_=pm[:, 384:512].rearrange("p (k d) -> p k d", k=2)),
               reads=[t_pmi], writes=[t_kv])
            transposes(kr, t_kr[ki], 4, [(KT_A[:, 4:6, ts_], 0, 2, [t_kv]), (KT_B[:, 0:2, ts_], 2, 4, [t_kv])])
        elif blk == 2:
            op(ACT, lambda e: e.copy(out=kr[:, 0:256], in_=pm[:, 0:256]), reads=[t_pmi], writes=[t_kr[ki]])
            op(ACT, lambda e: e.copy(out=VS[:, tt, :, 0:128], in_=pm[:, 256:512].rearrange("p (k d) -> p k d", k=2)),
               reads=[t_pmi], writes=[t_kv])
            transposes(kr, t_kr[ki], 2, [(VcaT[:, 0:2, ts_], 0, 2, [t_kv])])
        else:
            op(ACT, lambda e: e.copy(out=VW[:, tt, :, 0:128], in_=pm[:, 0:256].rearrange("p (k d) -> p k d", k=2)),
               reads=[t_pmi], writes=[t_kv])

    proj_pass([(QY, 0, 512, 0), (QY, 512, 512, 1), (QY, 1024, 512, 2), (QY, 1536, 256, 3)], post1, first=True, need_rt=True)
    barrier()
    W1V = [XS[0][:].bitcast(BF16).rearrange("p (l j) -> p l j", l=16),
           XS[1][:].bitcast(BF16).rearrange("p (l j) -> p l j", l=16)]
    wvv = cvw1_d.rearrange("(l d) j -> d l j", d=128)
    for hf in range(2):
        for l0 in range(0, 16, 8):
            dma(POOL, d_slabb[4], W1V[hf][:, l0:l0 + 8, :], wvv[:, hf * 16 + l0:hf * 16 + l0 + 8, :],
                writes=[t_slabb[4]])
    W1L = [W1K_L, [W1V[l // 16][:, l % 16, :] for l in range(32)]]
    t_w1 = [t_w1k, t_slabb[4]]
    SL2 = flat(QY[:, 8:16, :]).rearrange("p (k c) -> p k c", k=16)
    load_slab(SL2, wq1_d, 0, 1024)
    load_slab(GAW, wq1_d, 1024, 24, blk0=2)

    op(DVE, lambda e: e.memset(VCX[:], 0.0), writes=[t_kc])
    op(DVE, lambda e: e.memset(KCT[:], 0.0), writes=[t_kc])
    op(DVE, lambda e: e.memset(VCX[:, :, 128:129], 1.0), writes=[t_kc])
    for k in range(2):
        dma(SP, d_const, VCX[:, k, 129:161], overlap_d, writes=[t_kc])
    BLK = [flat(QY[:, 2 * x:2 * x + 2, :])[:, 0:4064].rearrange("p (l c) -> p l c", l=32) for x in range(4)]
    HID = [SCR[:, i * 128:i * 128 + 127] for i in range(8)]
    for x, (src, h, pos) in enumerate(((KT_A, 0, KPOS), (KT_A, 1, KPOS), (VcaT, 0, VPOS), (VcaT, 1, VPOS))):
        s2 = src[:, h, :]
        a = s2.ap
        srcv = bass.AP(s2.tensor, s2.offset, [list(a[0]), [1, 32], [16, 127]])
        if x % 2 == 0:
            op(DVE, lambda e, x=x, srcv=srcv, pos=pos: e.tensor_tensor(out=BLK[x], in0=srcv, in1=bc_last(pos[:, 0:32], 127), op=ALU.add),
               reads=[t_const], writes=[t_blk[x]])
        else:
            grp(ACT, [(lambda e, x=x, l=l, s2=s2, a=a, pos=pos: e.activation(
                out=BLK[x][:, l, :], in_=bass.AP(s2.tensor, s2.offset + l, [list(a[0]), [16, 127]]),
                func=ACTF.Identity, bias=pos[:, l:l + 1])) for l in range(32)],
                reads=[t_const], writes=[t_blk[x]])
    for x in range(4):
        kv = x // 2
        for jc in range(2):
            pi = pm_count[0] % 4
            pm_count[0] += 1
            grp(PE, [(lambda e, l=l: e.matmul(PM[pi][:, 0:127], lhsT=W1L[kv][l][:, jc * 128:(jc + 1) * 128],
                                              rhs=BLK[x][:, l, :], start=(l == 0), stop=(l == 31)))
                     for l in range(32)],
                reads=[t_w1[kv], t_blk[x]], writes=[t_pm[pi]])
            op(ACT, lambda e: e.activation(out=HID[x * 2 + jc], in_=PM[pi][:, 0:127], func=ACTF.Silu),
               reads=[t_pm[pi]], writes=[t_hid[x * 2 + jc]])
    for x in range(4):
        h = x % 2
        pi = pm_count[0] % 4
        pm_count[0] += 1
        if x < 2:
            grp(PE, [(lambda e, jc=jc: e.matmul(PM[pi][:, 0:127], lhsT=W2K[:, jc, :], rhs=HID[x * 2 + jc],
                                                start=(jc == 0), stop=(jc == 1))) for jc in range(2)],
                reads=[t_w1k, t_hid[x * 2], t_hid[x * 2 + 1]], writes=[t_pm[pi]])
            op(ACT, lambda e: e.copy(out=KCT[:, h, 0:127], in_=PM[pi][:, 0:127]), reads=[t_pm[pi]], writes=[t_kc])
        else:
            grp(PE, [(lambda e, jc=jc: e.matmul(PM[pi][0:127, 0:128], lhsT=HID[x * 2 + jc], rhs=W2V[:, jc, :],
                                                start=(jc == 0), stop=(jc == 1))) for jc in range(2)],
                reads=[t_w1k, t_hid[x * 2], t_hid[x * 2 + 1]], writes=[t_pm[pi]])
            op(ACT, lambda e: e.copy(out=VCX[0:127, h, 0:128], in_=PM[pi][0:127, 0:128]), reads=[t_pm[pi]], writes=[t_kc])
    barrier()

    if "kv" in debug:
        dd = dout("dbg_KCT", [128, 2, 128], BF16)
        dma(SP, d_out[0], dd, KCT[:], reads=[t_kc])
        dd = dout("dbg_VCX", [128, 2, 168], BF16)
        dma(SP, d_out[0], dd, VCX[:], reads=[t_kc])
        barrier()


    def post2(tt, blk, pm, t_pmi, w, b):
        ts_ = slice(tt * 128, (tt + 1) * 128)
        Q = tt // 4
        if blk < 2:
            ki = kr_cnt[0] % 4
            kr = KR[ki]
            rope(pm, t_pmi, 0, 4, 128, b, [(kr[:, 0:512].rearrange("p (h d) -> p h d", h=4), t_kr[ki])])
            transposes(kr, t_kr[ki], 4, [(QY[:, blk * 4:blk * 4 + 4, ts_], 0, 4, [t_q[blk * 4 + j][Q] for j in range(4)])])
        else:
            op(ACT, lambda e: e.activation(out=GATES[:, tt, :], in_=pm[:, 0:24], func=ACTF.Sigmoid),
               reads=[t_pmi], writes=[t_gates])

    proj_pass([(SL2, 0, 512, 0), (SL2, 512, 512, 1), (GAW, 0, 24, 2)], post2, need_rt=True)
    barrier()

    SCB = [0, 1, 2, 7]
    PVB = [(3, 4), (5, 6)]
    TRB = 7
    LA = 4
    st_cnt = [0]
    un_cnt = [0]

    class Unit:
        pass

    def run_units(units):
        flat = []
        for u in units:
            u.pair = un_cnt[0] % 2
            un_cnt[0] += 1
            u.first = [True, True]
            for si in range(len(u.steps)):
                flat.append((u, si))
        n = len(flat)
        slots = {}
        for idx in range(n + LA):
            if idx < n:
                u, si = flat[idx]
                g = st_cnt[0]
                st_cnt[0] += 1
                slots[idx] = g
                if si == 0 and getattr(u, "pre", None) is not None:
                    u.pre()
                u.emit_s(u, u.steps[si], g, g % 5)
            j = idx - LA
            if j >= 0:
                u, si = flat[j]
                g = slots.pop(j)
                u.emit_pv(u, u.steps[si], g % 5)
                if si == len(u.steps) - 1:
                    u.finish(u)

    inv_sqrt_a = 1.0 / math.sqrt(128.0)

    def a_emit_s(u, st, gstep, xi):
        segs, masks = st
        sb_ = SCB[gstep % 4]
        pmS = BK[sb_]
        kr_ = u.krows
        LO = min(sg[1] for sg in segs)
        HI = max(sg[2] for sg in segs)
        fns = []
        for (kt, lo, hi) in segs:
            fns.append(lambda e, kt=kt, lo=lo, hi=hi: e.matmul(pmS[0:kr_, lo:hi], lhsT=u.lhsT(kt), rhs=u.q(lo, hi),
                                                             start=True, stop=(u.bias is None)))
            if u.bias is not None:
                bl, br_ = u.bias(kt, lo, hi)
                fns.append(lambda e, lo=lo, hi=hi, bl=bl, br_=br_: e.matmul(pmS[0:kr_, lo:hi], lhsT=bl, rhs=br_,
                                                                           start=False, stop=True))
        grp(PE, fns, reads=[u.tq, t_selb, t_const], writes=[t_bk[sb_]])
        px = PX[xi]
        op(ACT, lambda e: e.activation(out=px[0:kr_, LO:HI], in_=pmS[0:kr_, LO:HI], func=ACTF.Exp, scale=u.scale),
           reads=[t_bk[sb_]], writes=[t_px[xi]])
        for (i, m) in masks:
            mk = TRI if m == "tri" else ATRI
            ME = POOL
            op(ME, lambda e, i=i, mk=mk: e.tensor_tensor(out=px[:, i * 128:(i + 1) * 128],
                                                         in0=px[:, i * 128:(i + 1) * 128],
                                                         in1=mk[:], op=ALU.mult),
               reads=[t_const], writes=[t_px[xi]])

    def a_emit_pv(u, st, xi):
        segs, masks = st
        px = PX[xi]
        banks = [BK[PVB[u.pair][0]], BK[PVB[u.pair][1]]]
        tb = [t_bk[PVB[u.pair][0]], t_bk[PVB[u.pair][1]]]
        fns = []
        used = set()
        for (kt, lo, hi) in segs:
            vv = u.v(kt)
            vw = vv.shape[-1]
            for i in range(lo // 128, hi // 128):
                bk = i // 2
                stf = u.first[bk]
                u.first[bk] = False
                used.add(bk)
                fns.append(lambda e, i=i, bk=bk, stf=stf, vv=vv, vw=vw: e.matmul(
                    banks[bk][:, (i % 2) * 256:(i % 2) * 256 + vw], lhsT=px[0:u.krows, i * 128:(i + 1) * 128],
                    rhs=vv, start=stf, stop=False, skip_group_check=True))
        grp(PE, fns, reads=[t_px[xi], t_kv, t_kc], writes=[tb[bk] for bk in sorted(used)])

    def a_finish(u):
        pr = u.pair
        tb = [t_bk[PVB[pr][0]], t_bk[PVB[pr][1]]]
        h, Q, br, g = u.h, u.Q, u.br, u.g
        pv4 = PVT[pr][:].rearrange("p (t c) -> p t c", t=4)
        if br == 0:
            op(DVE, lambda e: e.tensor_scalar(out=RD[:, 0:4], in0=pv4[:, :, 128], scalar1=TINY, scalar2=None, op0=ALU.max),
               reads=tb, writes=[t_rd])
            op(DVE, lambda e: e.reciprocal(out=RD2[:, 0:4], in_=RD[:, 0:4]), writes=[t_rd])
        else:
            op(DVE, lambda e: e.reciprocal(out=RD2[:, 0:4], in_=pv4[:, :, 128]), reads=tb, writes=[t_rd])
        op(DVE, lambda e: e.tensor_tensor(out=GW[:], in0=RD2[:, 0:4], in1=GATES[:, 4 * Q:4 * Q + 4, h * 3 + br], op=ALU.mult),
           reads=[t_gates], writes=[t_rd])
        if u.first_branch:
            op(DVE, lambda e: e.tensor_tensor(out=ACC[g], in0=pv4[:, :, 0:128], in1=bc_last(GW[:], 128), op=ALU.mult),
               reads=tb + [t_rd], writes=[t_acc[g]])
        else:
            op(DVE, lambda e: e.tensor_tensor(out=TMPA, in0=pv4[:, :, 0:128], in1=bc_last(GW[:], 128), op=ALU.mult),
               reads=tb + [t_rd], writes=[t_tmpa])
            if br == 1:
                op(DVE, lambda e: e.tensor_tensor(out=QY[:, h, Q * 512:(Q + 1) * 512].rearrange("p (t d) -> p t d", t=4),
                                                  in0=TMPA, in1=ACC[g], op=ALU.add),
                   reads=[t_tmpa, t_acc[g]], writes=[u.tq])
            else:
                op(DVE, lambda e: e.tensor_tensor(out=ACC[g], in0=TMPA, in1=ACC[g], op=ALU.add),
                   reads=[t_tmpa], writes=[t_acc[g]])
        if u.want_psel:
            if g == 0:
                op(DVE, lambda e: e.tensor_tensor(out=PSEL[:], in0=pv4[:, :, 129:161], in1=bc_last(RD2[:, 0:4], 32), op=ALU.mult),
                   reads=tb + [t_rd], writes=[t_sel])
            else:
                op(DVE, lambda e: e.tensor_tensor(out=TMPS[:], in0=pv4[:, :, 129:161], in1=bc_last(RD2[:, 0:4], 32), op=ALU.mult),
                   reads=tb + [t_rd], writes=[t_tmpa])
                op(DVE, lambda e: e.tensor_tensor(out=PSEL[:], in0=PSEL[:], in1=TMPS[:], op=ALU.add),
                   reads=[t_tmpa], writes=[t_sel])
        if u.after is not None:
            u.after()

    def selection(k, Q):
        op(DVE, lambda e: e.tensor_tensor(out=SCO[:], in0=PSEL[:], in1=SELCQ[:, :, 0:32], op=ALU.mult),
           reads=[t_selc], writes=[t_sel])
        op(DVE, lambda e: e.tensor_tensor(out=SCO[:], in0=SCO[:], in1=SELCQ[:, :, 32:64], op=ALU.add),
           reads=[t_selc], writes=[t_sel])
        for i in range(4):
            op(DVE, lambda e: e.max(out=M8[:], in_=SCO[:, i, :]), writes=[t_sel])
            op(DVE, lambda e: e.match_replace(out=SC2[:], in_to_replace=M8[:], in_values=SCO[:, i, :],
                                              imm_value=-2.0), writes=[t_sel])
            op(DVE, lambda e: e.max(out=M8b[:], in_=SC2[:]), writes=[t_sel])
            op(DVE, lambda e: e.tensor_scalar(out=SC2[:], in0=SCO[:, i, :], scalar1=M8b[:, 7:8], scalar2=None,
                                              op0=ALU.is_lt), writes=[t_sel])
            op(DVE, lambda e: e.tensor_scalar(out=NB[:, i, :], in0=SC2[:], scalar1=NEGBIG, scalar2=None,
                                              op0=ALU.mult), writes=[t_sel])
        grp(PE, [(lambda e, i=i: e.transpose(out=BKb[TRB][0:32, i * 128:(i + 1) * 128], in_=NB[:, i, :],
                                              identity=IDENT[:])) for i in range(4)],
            reads=[t_sel, t_const], writes=[t_bk[TRB]])
        op(ACT, lambda e: e.copy(out=SELB[0:32, k, (Q - 2) * 512:(Q - 1) * 512], in_=BKb[TRB][0:32, 0:512]),
           reads=[t_bk[TRB]], writes=[t_selb])

    def mk_a_unit(k, Q, g, br):
        u = Unit()
        h = 4 * k + g
        u.h, u.Q, u.br, u.g, u.k = h, Q, br, g, k
        u.tq = t_q[h][Q]
        u.q = lambda lo, hi: QY[:, h, Q * 512 + lo:Q * 512 + hi]
        u.scale = inv_sqrt_a
        u.bias = None
        u.krows = 128
        u.after = None
        u.want_psel = False
        u.first_branch = False
        u.emit_s, u.emit_pv, u.finish = a_emit_s, a_emit_pv, a_finish
        if br == 0:
            u.steps = [([(0, 0, 512)], [])]
            u.bias = lambda kt, lo, hi: (IDENT[:, 0:127], CMASK[:, Q * 512 + lo:Q * 512 + hi])
            u.lhsT = lambda kt: KCT[:, k, 0:127]
            u.v = lambda kt: VCX[0:127, k, 0:161]
            u.krows = 127
            u.first_branch = True
            u.want_psel = Q >= 2
            if g == 3 and Q >= 2:
                u.after = lambda: selection(k, Q)
        elif br == 2:
            steps = []
            if Q == 0:
                for r in range(0, 4):
                    steps.append(([(r, r * 128, 512)], [(r, "tri")]))
            else:
                kt0 = 4 * Q
                for j in range(3):
                    steps.append(([(kt0 - 4 + j, 0, (j + 1) * 128), (kt0 + 1 + j, (j + 1) * 128, 512)],
                                  [(j, "atri"), (j + 1, "tri")]))
                steps.append(([(kt0 - 1, 0, 512)], [(3, "atri")]))
                steps.append(([(kt0, 0, 512)], [(0, "tri")]))
            u.steps = steps
            u.lhsT = lambda kt: KT_A[:, 4 + k, kt * 128:(kt + 1) * 128]
            u.v = lambda kt: VW[:, kt, k, 0:129]
        else:
            steps = []
            for kt in range(0, 4 * Q + 4):
                r = kt - 4 * Q
                if r < 0:
                    steps.append(([(kt, 0, 512)], []))
                else:
                    steps.append(([(kt, r * 128, 512)], [(r, "tri")]))
            u.steps = steps
            u.lhsT = lambda kt: KT_A[:, 2 + k, kt * 128:(kt + 1) * 128]
            u.v = lambda kt: VS[:, kt, k, 0:129]
            if Q >= 2:
                u.bias = lambda kt, lo, hi: (ESEL[:, kt * 128:(kt + 1) * 128],
                                             SELB[:, k, (Q - 2) * 512 + lo:(Q - 2) * 512 + hi])
        return u

    groups = [(k, Q) for k in range(2) for Q in range(NQG)]

    def cmp_unit(n, g):
        k, Q = groups[n]
        u = mk_a_unit(k, Q, g, 0)
        if g == 0 and Q >= 2:
            u.pre = lambda: dma(SP, d_selc, SELCQ[:], selc_d[4 * Q:4 * Q + 4].rearrange("t p c -> p t c"),
                                writes=[t_selc])
        return u

    wv_za = wq2_d.rearrange("(kc p) c -> p kc c", p=128)

    def za_piece(j, kc):
        return lambda: dma(POOL, d_slabb[j], SL2[:, kc:kc + 8, j * 512:(j + 1) * 512],
                           wv_za[:, kc:kc + 8, j * 512:(j + 1) * 512], writes=[t_slabb[j]])
    za_pieces = [za_piece(j, kc) for j in range(2) for kc in (0, 8)]
    units = [cmp_unit(0, g) for g in range(4)]
    for n, (k, Q) in enumerate(groups):
        if Q >= 2:
            for g in range(4):
                units.append(mk_a_unit(k, Q, g, 2))
        for g in range(4):
            if Q < 2:
                units.append(mk_a_unit(k, Q, g, 2))
            usel = mk_a_unit(k, Q, g, 1)
            if k == 0 and Q == 2:
                usel.pre = za_pieces[g]
            units.append(usel)
            if n + 1 < len(groups):
                units.append(cmp_unit(n + 1, g))
    run_units(units)
    barrier()

    if "oa" in debug:
        dd = dout("dbg_OA", [128, 8, S], BF16)
        dma(SP, d_out[0], dd, QY[:, 0:8, :])
        dd = dout("dbg_GATES", [128, NT, 24], F32)
        dma(SP, d_out[0], dd, GATES[:])
        barrier()

    def post_z(cbase):
        def f(tt, blk, pm, t_pmi, w, b):
            ts_ = slice(tt * 128, (tt + 1) * 128)
            Q = tt // 4
            cb = cbase + blk * 4
            tl = [t_q[cb + j][Q] for j in range(4)]
            op(ACT, lambda e: e.activation(out=TA[:, 0:512], in_=pm[:, 0:512], func=ACTF.Silu),
               reads=[t_pmi], writes=[t_ta])
            ki = kr_cnt[0] % 4
            kr = KR[ki]
            op(DVE, lambda e: e.tensor_tensor(out=kr[:, 0:512].rearrange("p (c d) -> p c d", c=4),
                                              in0=TA[:, 0:512].rearrange("p (c d) -> p c d", c=4),
                                              in1=QY[:, cb:cb + 4, ts_], op=ALU.mult),
               reads=[t_ta] + tl, writes=[t_kr[ki]])
            transposes(kr, t_kr[ki], 4, [(QY[:, cb:cb + 4, ts_], 0, 4, tl)])
        return f

    load_slab(RLO, wq2_d, 1024, 1024, blk0=2)
    pz3 = post_z(0)

    def post3(tt, blk, pm, t_pmi, w, b):
        if blk < 2:
            pz3(tt, blk, pm, t_pmi, w, b)
        else:
            ts_ = slice(tt * 128, (tt + 1) * 128)
            Q = tt // 4
            ki = kr_cnt[0] % 4
            kr = KR[ki]
            cb = 8 + (blk - 2) * 4
            rope(pm, t_pmi, 0, 8, 64, b, [(kr[:, 0:512].rearrange("p (h d) -> p h d", h=8), t_kr[ki])])
            transposes(kr, t_kr[ki], 4, [(R2[:, cb:cb + 4, ts_], 0, 4, [t_q[cb + j][Q] for j in range(4)])])

    proj_pass([(SL2, 0, 512, 0), (SL2, 512, 512, 1), (RLO, 0, 512, 2), (RLO, 512, 512, 3)], post3, need_rt=True)
    barrier()

    load_slab(RLO, wz2_d, 0, 1024)
    op(ACT, lambda e: e.activation(out=ESINK[:], in_=ESINK[:], func=ACTF.Exp), writes=[t_const])

    KBZ = [XS[0][:].bitcast(BF16).rearrange("p (k t) -> p k t", k=2), XS[1][:].bitcast(BF16).rearrange("p (k t) -> p k t", k=2)]
    op(DVE, lambda e: e.memset(KBZ[0][64:128, :, :], 0.0), writes=[t_xs[0]])
    op(DVE, lambda e: e.memset(KBZ[1][0:64, :, :], 0.0), writes=[t_xs[1]])
    op(DVE, lambda e: e.tensor_copy(out=KBZ[0][0:64, :, :], in_=KT_B[0:64, :, :]), reads=[t_kv], writes=[t_xs[0]])
    op(ACT, lambda e: e.copy(out=KBZ[1][64:128, :, :], in_=KT_B[64:128, :, :]), reads=[t_kv], writes=[t_xs[1]])

    def b_emit_s(u, st, gstep, xi):
        kt, lo, hi, mk = st
        w = hi - lo
        sbs = [(0, 1), (2, 7)][gstep % 2]
        px = PX[xi]
        pxv = px[:, 0:512].rearrange("p (e c) -> p e c", e=2)[:, :, 0:w]
        fns = []
        for e_ in range(2):
            fns.append(lambda e, e_=e_: e.matmul(BK[sbs[e_]][:, 0:w], lhsT=KBZ[e_][:, u.kvb, kt * 128:(kt + 1) * 128],
                                                 rhs=R2[:, 8 + u.i, u.Q * 512 + lo:u.Q * 512 + hi],
                                                 start=True, stop=not B_PE_BIAS))
        if B_PE_BIAS:
            for e_ in range(2):
                fns.append(lambda e, e_=e_: e.matmul(BK[sbs[e_]][:, 0:w], lhsT=IDENT[:], rhs=mk, start=False, stop=True))
        grp(PE, fns, reads=[u.tq, t_xs[0], t_xs[1], t_const], writes=[t_bk[sbs[0]], t_bk[sbs[1]]])
        for e_ in range(2):
            pmS = BK[sbs[e_]]
            op(ACT, lambda e: e.activation(out=px[:, e_ * 256:e_ * 256 + w], in_=pmS[:, 0:w],
                                           func=ACTF.Exp, scale=0.125),
               reads=[t_bk[sbs[e_]]], writes=[t_px[xi]])
        if not B_PE_BIAS:
            pxv = px[:, 0:512].rearrange("p (e c) -> p e c", e=2)[:, :, 0:w]
            op(DVE, lambda e: e.tensor_tensor(out=pxv, in0=pxv, in1=bc_mid(u.mk01(st), 2), op=ALU.mult),
               reads=[t_const], writes=[t_px[xi]])

    def b_emit_pv(u, st, xi):
        kt, lo, hi, mk = st
        px = PX[xi]
        fns = []
        for e_ in range(2):
            bank = BK[PVB[u.pair][e_]]
            for j in range((hi - lo) // 128):
                i = lo // 128 + j
                stf = u.first[e_]
                u.first[e_] = False
                fns.append(lambda e, e_=e_, j=j, i=i, stf=stf, bank=bank: e.matmul(
                    bank[:, i * 128:i * 128 + 65], lhsT=px[:, e_ * 256 + j * 128:e_ * 256 + (j + 1) * 128],
                    rhs=VB[:, kt, u.kvb, 0:65], start=stf, stop=False, skip_group_check=True))
        grp(PE, fns, reads=[t_px[xi], t_kv], writes=[t_bk[PVB[u.pair][0]], t_bk[PVB[u.pair][1]]])

    def b_finish(u):
        pr = u.pair
        tb = [t_bk[PVB[pr][0]], t_bk[PVB[pr][1]]]
        pvb = PVT[pr][:].rearrange("p (e t c) -> p e t c", e=2, t=4)
        op(DVE, lambda e: e.tensor_tensor(out=RD[:].rearrange("p (e t) -> p e t", e=2), in0=pvb[:, :, :, 64],
                                          in1=bc_last(ESINK[:, 2 * u.i:2 * u.i + 2], 4), op=ALU.add),
           reads=tb + [t_const], writes=[t_rd])
        op(DVE, lambda e: e.reciprocal(out=RD2[:], in_=RD[:]), writes=[t_rd])
        qc = slice(u.Q * 512, (u.Q + 1) * 512)
        op(DVE, lambda e: e.tensor_tensor(out=QY[:, 8 + u.i, qc].rearrange("p (t e d) -> p e t d", t=4, e=2),
                                          in0=pvb[:, :, :, 0:64],
                                          in1=bc_last(RD2[:].rearrange("p (e t) -> p e t", e=2), 64), op=ALU.mult),
           reads=tb + [t_rd], writes=[u.tq])

    bunits = []
    for i in range(8):
        for Q in range(NQG):
            u = Unit()
            u.i, u.Q, u.kvb = i, Q, i // 4
            u.tq = t_q[8 + i][Q]
            steps = []
            for r in range(-1, 4):
                kt = 4 * Q + r
                if kt < 0:
                    continue
                if r < 0:
                    steps.append((kt, 0, 128, TRAT[:, 128:256]))
                elif r < 3:
                    steps.append((kt, r * 128, (r + 2) * 128, TRAT[:]))
                else:
                    steps.append((kt, 384, 512, TRAT[:, 0:128]))
            u.steps = steps
            u.mk01 = lambda st: (ATRI[:] if st[2] - st[1] == 128 and st[1] == 0 else (TRI[:] if st[2] - st[1] == 128 else TRAT01[:]))
            u.emit_s, u.emit_pv, u.finish = b_emit_s, b_emit_pv, b_finish
            bunits.append(u)
    run_units(bunits)
    barrier()

    if "ob" in debug:
        dd = dout("dbg_OB", [128, 8, S], BF16)
        dma(SP, d_out[0], dd, QY[:, 8:16, :])
        barrier()

    load_slab(RHI, wout_d, 1024, 1024, blk0=2)
    proj_pass([(RLO, 0, 512, 0), (RLO, 512, 512, 1)], post_z(8))
    barrier()

    load_slab(RLO, wout_d, 0, 1024)
    FG = flat(KT_B[:]).bitcast(F32)
    dma(SP, d_const, FG, fg_d, writes=[t_const])
    xv = x_d.rearrange("(t p) d -> t p d", p=128)
    ov = out_d.rearrange("(t p) d -> t p d", p=128)
    def epilogue(tt, b):
        def run():
            op(ACT, lambda e: e.activation(out=HB[:], in_=XS[b][:], func=ACTF.Square, accum_out=SS2[:, tt:tt + 1]),
               reads=[t_xs[b]], writes=[t_hb, t_rstd])
            op(ACT, lambda e: e.activation(out=SS2[:, tt:tt + 1], in_=SS2[:, tt:tt + 1], func=ACTF.Sqrt,
                                           scale=1.0 / D, bias=EPS_T[:, 0:1]), writes=[t_rstd])
            op(DVE, lambda e: e.reciprocal(out=RS2[:, tt:tt + 1], in_=SS2[:, tt:tt + 1]), writes=[t_rstd])
            op(DVE, lambda e: e.scalar_tensor_tensor(out=XS[b][:], in0=XS[b][:], scalar=RS2[:, tt:tt + 1], in1=FG,
                                                     op0=ALU.mult, op1=ALU.mult),
               reads=[t_rstd, t_const], writes=[t_xs[b]])
            dma(SP, d_out[b], ov[tt], XS[b][:], reads=[t_xs[b]])
        return run

    p5_cnt = 0
    sched = [(0, 2), (0, 3), (1, 2), (1, 3), (0, 0), (0, 1), (1, 0), (1, 1)]
    for tt in range(2, NT):
        sched += [(tt, 2), (tt, 3), (tt, 0), (tt, 1)]
    done_blocks = {}
    loaded = set()
    epi_q = []
    for (tt, blk) in sched:
        b = tt % 2
        ts_ = slice(tt * 128, (tt + 1) * 128)
        if tt not in loaded:
            loaded.add(tt)
            dma(SP, d_xs[b], XS[b][:], xv[tt], writes=[t_xs[b]])
        pi = p5_cnt % 8
        p5_cnt += 1
        grp(PE, [(lambda e, c=c: e.matmul(BK[pi][:, 0:512], lhsT=QY[:, c, ts_],
                                          rhs=(RLO if blk < 2 else RHI)[:, c, (blk % 2) * 512:(blk % 2 + 1) * 512],
                                          start=(c == 0), stop=(c == 15))) for c in range(16)],
            reads=[t_slabb[blk]], writes=[t_bk[pi]])
        op(DVE, lambda e: e.tensor_tensor(out=XS[b][:, blk * 512:(blk + 1) * 512], in0=BK[pi][:, 0:512],
                                          in1=XS[b][:, blk * 512:(blk + 1) * 512], op=ALU.add),
           reads=[t_bk[pi]], writes=[t_xs[b]])
        done_blocks[tt] = done_blocks.get(tt, 0) + 1
        for item in epi_q:
            item[1] -= 1
        while epi_q and epi_q[0][1] <= 0:
            epi_q.pop(0)[0]()
        if done_blocks[tt] == 4:
            epi_q.append([epilogue(tt, b), 2])
    for item in epi_q:
        item[0]()

    for d in (d_out[0], d_out[1]):
        if d.n > 0:
            SP.wait(Ev(d, d.n))
    return nc, ctx


def make_in_maps(x, w_in, cmp_k_w1, cmp_k_w2, cmp_v_w1, cmp_v_w2, cmp_k_pos, cmp_v_pos,
                 sinks, w_out, norm_g, final_g, n_cores=8):
    c = _consts()
    w_kv, w_q1, w_q2, w_z2 = _split_w_in(np.asarray(w_in[0], np.float32))
    shared = {
        "w_kv": w_kv, "w_q1": w_q1, "w_q2": w_q2, "w_z2": w_z2,
        "w_out": np.ascontiguousarray(w_out[0], np.float32),
        "cmp_k_w1": np.ascontiguousarray(cmp_k_w1[0], np.float32),
        "cmp_k_w2": np.ascontiguousarray(cmp_k_w2[0], np.float32),
        "cmp_v_w1": np.ascontiguousarray(cmp_v_w1[0], np.float32),
        "cmp_v_w2": np.ascontiguousarray(cmp_v_w2[0], np.float32),
        "kposT": np.ascontiguousarray(np.asarray(cmp_k_pos[0], np.float32).T),
        "vposT": np.ascontiguousarray(np.asarray(cmp_v_pos[0], np.float32).T),
        "sinks_bc": np.ascontiguousarray(np.broadcast_to(np.asarray(sinks[0], np.float32)[None, :], (128, 16))),
        "g_col": np.ascontiguousarray(np.asarray(norm_g[0], np.float32).reshape(16, 128).T),
        "fg_bc": np.ascontiguousarray(np.broadcast_to(np.asarray(final_g, np.float32)[None, :], (128, D))),
    }
    shared.update(c)
    in_maps = []
    for b in range(n_cores):
        m = dict(shared)
        m["x"] = np.ascontiguousarray(x[b], np.float32)
        in_maps.append(m)
    return in_maps


def kernel(**inputs):
    inputs = {k: np.asarray(v) for k, v in inputs.items()}
    in_maps = make_in_maps(**inputs)
    nc, ctx = build()
    with ctx:
        res = run_bass_kernel_spmd(nc, in_maps, core_ids=list(range(8)))
    outs = [np.asarray(r["out"], np.float32) for r in res.results]
    return np.stack(outs, axis=0)
```

```python
import math
from contextlib import ExitStack

import numpy as np
import ml_dtypes

import concourse.bass as bass
import concourse.mybir as mybir
from concourse.bass_utils import run_bass_kernel_spmd

F32 = mybir.dt.float32
BF16 = mybir.dt.bfloat16
ALU = mybir.AluOpType
ACTF = mybir.ActivationFunctionType
AX = mybir.AxisListType

S = 2048
D = 2048
NT = 16
NQG = 4
RMS_EPS = 1e-6
TINY = 1e-30
NEGBIG = -30000.0
B_PE_BIAS = False


class DmaSem:
    def __init__(self, sem):
        self.sem = sem
        self.n = 0


class Ev:
    __slots__ = ("holder", "val")

    def __init__(self, holder, val):
        self.holder = holder
        self.val = val


class Eng:
    def __init__(self, nc, ctx, eng, name):
        self.nc = nc
        self.e = eng
        self.name = name
        self.sem = ctx.enter_context(nc.semaphore("sem_" + name))
        self.n = 0
        self.waited = {}

    def wait(self, ev):
        if ev is None:
            return
        h = ev.holder
        val = h.n if isinstance(h, DmaSem) else ev.val
        key = id(h)
        if self.waited.get(key, 0) >= val:
            return
        self.waited[key] = val
        self.e.wait_ge(h.sem, val)

    def done(self, inst):
        self.n += 1
        inst.then_inc(self.sem, 1)
        return Ev(self, self.n)


class T:
    def __init__(self, name="", excl=False):
        self.name = name
        self.w = None
        self.r = []
        self.excl = excl


def _split(reads, writes):
    ex = [t for t in reads if t.excl]
    if ex:
        reads = [t for t in reads if not t.excl]
        writes = list(writes) + [t for t in ex if t not in writes]
    return reads, writes


def _pre(E, reads, writes):
    for t in reads:
        E.wait(t.w)
    for t in writes:
        E.wait(t.w)
        for r in t.r:
            E.wait(r)


def _post(ev, reads, writes):
    for t in reads:
        t.r.append(ev)
    for t in writes:
        t.w = ev
        t.r = []


def op(E, fn, reads=(), writes=()):
    reads, writes = _split(reads, writes)
    _pre(E, reads, writes)
    inst = fn(E.e)
    ev = E.done(inst)
    _post(ev, reads, writes)
    return ev


def grp(E, fns, reads=(), writes=()):
    reads, writes = _split(reads, writes)
    _pre(E, reads, writes)
    inst = None
    for fn in fns:
        inst = fn(E.e)
    ev = E.done(inst)
    _post(ev, reads, writes)
    return ev


def dma(E, dsem, out, in_, reads=(), writes=(), **kw):
    _pre(E, reads, writes)
    E.e.dma_start(out=out, in_=in_, **kw).then_inc(dsem.sem, 16)
    dsem.n += 16
    ev = Ev(dsem, dsem.n)
    _post(ev, reads, writes)
    return ev


def bc_mid(ap2d, n):
    a = ap2d.ap
    return bass.AP(ap2d.tensor, ap2d.offset, [list(a[0]), [0, n]] + [list(x) for x in a[1:]])


def bc_last(ap2d, n):
    a = ap2d.ap
    return bass.AP(ap2d.tensor, ap2d.offset, [list(x) for x in a] + [[0, n]])


def _consts():
    bf = ml_dtypes.bfloat16
    c = {}
    pos = np.arange(S, dtype=np.float32)

    def tab(d):
        inv = (np.float32(10000.0) ** (-(np.arange(0, d, 2, dtype=np.float32)) / np.float32(d))).astype(np.float32)
        ang = (pos[:, None] * inv[None, :]).astype(np.float32)
        return np.cos(ang).astype(np.float32), np.sin(ang).astype(np.float32)

    cA, sA = tab(128)
    cB, sB = tab(64)
    rt = np.concatenate([cA, cA, -sA, sA, cB, cB, -sB, sB], axis=1).astype(np.float32)
    c["ropetab"] = np.ascontiguousarray(rt.reshape(NT, 128, 384))
    c["ident"] = np.eye(128, dtype=np.float32).astype(bf)
    p = np.arange(128)
    c["tri"] = (p[:, None] <= p[None, :]).astype(np.float32).astype(bf)
    c["atri"] = (p[None, :] < p[:, None]).astype(np.float32).astype(bf)
    vis = np.concatenate([(p[:, None] <= p[None, :]), (p[None, :] < p[:, None])], axis=1).astype(np.float32)
    c["ntrat"] = ((vis - 1.0) * 30000.0).astype(bf)
    cc = np.arange(128)
    cm = ((cc[:, None] * 16 + 31) <= np.arange(S)[None, :]).astype(np.float32)
    cm[127, :] = 0.0
    c["cmask"] = ((cm - 1.0) * 30000.0).astype(bf)
    cs = np.arange(127) * 16
    js = np.arange(32) * 64
    ov = ((cs[:, None] < js[None, :] + 64) & (cs[:, None] + 32 > js[None, :])).astype(np.float32)
    ovp = np.zeros((128, 32), np.float32)
    ovp[:127] = ov
    c["overlap"] = ovp.astype(bf)
    es = (np.arange(S)[None, :] // 64 == np.arange(32)[:, None]).astype(np.float32)
    esp = np.zeros((128, S), np.float32)
    esp[:32] = es
    c["esel"] = esp.astype(bf)
    t = np.arange(S)[:, None]
    j = np.arange(32)[None, :]
    cur = t // 64
    forced = (j == 0) | (j == cur) | (j == cur - 1)
    valid = (j * 64) <= t
    vmask = (valid & ~forced).astype(np.float32)
    cconst = np.where(forced, 1e4, np.where(valid, 0.0, -1.0)).astype(np.float32)
    sc = np.stack([vmask, cconst], axis=1)
    c["selc"] = np.ascontiguousarray(sc.reshape(NT, 128, 64))
    return c


def _split_w_in(w_in):
    sizes = (1024, 256, 256, 256, 256, 256, 256, 1024, 24, 1024, 128, 128, 1024)
    offs = np.cumsum((0,) + sizes)
    names = ["qa", "kca", "vca", "ksa", "vsa", "kwa", "vwa", "za", "ga", "qb", "kb", "vb", "zb"]
    cols = {n: w_in[:, offs[i]:offs[i + 1]] for i, n in enumerate(names)}
    w_kv = np.concatenate([cols["kca"], cols["ksa"], cols["kwa"], cols["kb"], cols["vb"],
                           cols["vca"], cols["vsa"], cols["vwa"]], axis=1)
    w_q1 = np.concatenate([cols["qa"], cols["ga"]], axis=1)
    w_q2 = np.concatenate([cols["za"], cols["qb"]], axis=1)
    w_z2 = cols["zb"]
    return (np.ascontiguousarray(w_kv), np.ascontiguousarray(w_q1),
            np.ascontiguousarray(w_q2), np.ascontiguousarray(w_z2))


def build(debug=None, stop_after=None):
    debug = debug or {}
    nc = bass.Bass("TRN2", target_bir_lowering=False)
    ctx = ExitStack()

    def din(name, shape, dt=F32):
        return nc.dram_tensor(name, list(shape), dt, kind="ExternalInput").ap()

    x_d = din("x", [S, D])
    wkv_d = din("w_kv", [D, 1792])
    wq1_d = din("w_q1", [D, 1048])
    wq2_d = din("w_q2", [D, 2048])
    wz2_d = din("w_z2", [D, 1024])
    wout_d = din("w_out", [D, D])
    ckw1_d = din("cmp_k_w1", [4096, 256])
    ckw2_d = din("cmp_k_w2", [256, 128])
    cvw1_d = din("cmp_v_w1", [4096, 256])
    cvw2_d = din("cmp_v_w2", [256, 128])
    kposT_d = din("kposT", [128, 32])
    vposT_d = din("vposT", [128, 32])
    sinks_d = din("sinks_bc", [128, 16])
    gcol_d = din("g_col", [128, 16])
    fg_d = din("fg_bc", [128, D])
    ropetab_d = din("ropetab", [NT, 128, 384])
    ident_d = din("ident", [128, 128], BF16)
    tri_d = din("tri", [128, 128], BF16)
    atri_d = din("atri", [128, 128], BF16)
    ntrat_d = din("ntrat", [128, 256], BF16)
    cmask_d = din("cmask", [128, S], BF16)
    overlap_d = din("overlap", [128, 32], BF16)
    esel_d = din("esel", [128, S], BF16)
    selc_d = din("selc", [NT, 128, 64])
    out_d = nc.dram_tensor("out", [S, D], F32, kind="ExternalOutput").ap()
    dbg_d = {}

    def dout(name, shape, dt=F32):
        dbg_d[name] = nc.dram_tensor(name, list(shape), dt, kind="ExternalOutput").ap()
        return dbg_d[name]

    PE = Eng(nc, ctx, nc.tensor, "pe")
    ACT = Eng(nc, ctx, nc.scalar, "act")
    DVE = Eng(nc, ctx, nc.vector, "dve")
    POOL = Eng(nc, ctx, nc.gpsimd, "pool")
    SP = Eng(nc, ctx, nc.sync, "sp")
    ENGS = [PE, ACT, DVE, POOL, SP]
    all_tiles = []
    all_dsems = []

    def TT(name, excl=False):
        t = T(name, excl)
        all_tiles.append(t)
        return t

    def dsem(name):
        d = DmaSem(ctx.enter_context(nc.semaphore("d_" + name)))
        all_dsems.append(d)
        return d

    def barrier():
        for E in ENGS:
            for F in ENGS:
                if F is not E and F.n > 0:
                    E.wait(Ev(F, F.n))
            for d in all_dsems:
                if d.n > 0:
                    E.wait(Ev(d, d.n))
        for t in all_tiles:
            t.w = None
            t.r = []

    def sb(name, shape, dt):
        return ctx.enter_context(nc.sbuf_tensor(name, list(shape), dt))

    def ps(name, shape, dt):
        return ctx.enter_context(nc.psum_tensor(name, list(shape), dt))

    def flat(ap3):
        return ap3.rearrange("p a b -> p (a b)")

    QY = sb("QY", [128, 16, S], BF16)
    R2 = sb("R2", [128, 16, S], BF16)
    RLO = flat(R2[:, 0:8, :]).rearrange("p (k c) -> p k c", k=16)
    RHI = flat(R2[:, 8:16, :]).rearrange("p (k c) -> p k c", k=16)
    KT_A = R2[:, 0:6, :]
    VcaT = R2[:, 6:8, :]
    VS = flat(R2[:, 8:11, :])[:, 0:4128].rearrange("p (t k d) -> p t k d", t=NT, k=2)
    VW = flat(R2[:, 11:14, :])[:, 0:4128].rearrange("p (t k d) -> p t k d", t=NT, k=2)
    KT_B = sb("KT_B", [128, 2, S], BF16)
    VB = sb("VB", [128, NT, 2, 65], BF16)
    XS = [sb(f"xs{i}", [128, D], F32) for i in range(2)]
    HB = sb("hb", [128, D], BF16)
    HT = [sb(f"hT{i}", [128, 16, 128], BF16) for i in range(2)]
    RT = [sb(f"rt{i}", [128, 384], F32) for i in range(2)]
    SCR = sb("scr", [128, 7168], BF16)
    TA = SCR[:, 0:1024].bitcast(F32)
    TB = SCR[:, 1024:2048].bitcast(F32)
    KR = [SCR[:, 2048 + i * 512:2560 + i * 512] for i in range(4)]
    PX = [SCR[:, i * 512:(i + 1) * 512] for i in range(4)] + [sb("px4", [128, 512], BF16)]
    ACC = [SCR[:, 2048 + g * 1024:3072 + g * 1024].bitcast(F32).rearrange("p (t d) -> p t d", t=4) for g in range(4)]
    OBF = SCR[:, 2048:2560].rearrange("p (t d) -> p t d", t=4)
    TMPA = SCR[:, 6144:7168].bitcast(F32).rearrange("p (t d) -> p t d", t=4)
    IDENT = sb("ident_s", [128, 128], BF16)
    TRI = sb("tri_s", [128, 128], BF16)
    ATRI = sb("atri_s", [128, 128], BF16)
    TRAT01 = sb("trat01_s", [128, 256], BF16)
    TRAT = TRAT01
    CMASK = sb("cmask_s", [128, S], BF16)
    ESEL = sb("esel_s", [128, S], BF16)
    SELB = sb("selb", [128, 2, 1024], BF16)
    GAW = sb("gaw", [128, 16, 24], BF16)
    GATES = sb("gates", [128, NT, 24], F32)
    KCT = sb("kct", [128, 2, 128], BF16)
    VCX = sb("vcx", [128, 2, 168], BF16)
    W2K = sb("w2k", [128, 2, 128], BF16)
    W2V = sb("w2v", [128, 2, 128], BF16)
    KPOS = sb("kpos", [128, 32], F32)
    VPOS = sb("vpos", [128, 32], F32)
    GCOL = sb("gcol", [128, 16], F32)
    RSTD = sb("rstd", [128, NT], F32)
    SS = sb("ss", [128, NT], F32)
    SS2 = sb("ss2", [128, NT], F32)
    RS2 = sb("rs2", [128, NT], F32)
    EPS_T = sb("eps_t", [128, 1], F32)
    ESINK = sb("esink", [128, 16], F32)
    SELCQ = sb("selcq", [128, 4, 64], F32)
    PSEL = sb("psel", [128, 4, 32], F32)
    SCO = sb("sco", [128, 4, 32], F32)
    SC2 = sb("sc2", [128, 32], F32)
    M8 = sb("m8", [128, 8], F32)
    M8b = sb("m8b", [128, 8], F32)
    NB = sb("nb", [128, 4, 32], BF16)
    RD = sb("rd", [128, 8], F32)
    RD2 = sb("rd2", [128, 8], F32)
    GW = sb("gw", [128, 4], F32)
    TMPS = sb("tmps", [128, 4, 32], F32)

    _b012 = [ps(f"bk{i}", [128, 512], F32) for i in range(3)]
    PVT = [ps(f"pvt{i}", [128, 1024], F32) for i in range(2)]
    _b7 = ps("bk7", [128, 512], F32)
    BK = [_b012[0][:], _b012[1][:], _b012[2][:], PVT[0][:, 0:512], PVT[0][:, 512:1024],
          PVT[1][:, 0:512], PVT[1][:, 512:1024], _b7[:]]
    BKb = [BK[i].bitcast(BF16) for i in range(8)]
    PM = BK[0:4]

    t_xs = [TT("xs0"), TT("xs1")]
    t_hb = TT("hb")
    t_ht = [TT("ht0"), TT("ht1")]
    t_rt = [TT("rt0"), TT("rt1")]
    t_ta, t_tb = TT("ta"), TT("tb")
    t_kr = [TT(f"kr{i}") for i in range(4)]
    t_bk = [TT(f"bk{i}", True) for i in range(8)]
    t_pm = t_bk[0:4]
    t_slabb = [TT(f"slab{i}") for i in range(5)]
    t_const = TT("const")
    t_rstd = TT("rstd")
    t_kv = TT("kv")
    t_q = [[TT(f"q{c}_{q}") for q in range(NQG)] for c in range(16)]
    t_gates = TT("gates")
    t_px = [TT(f"px{i}") for i in range(5)]
    t_acc = [TT(f"acc{g}") for g in range(4)]
    t_obf = TT("obf")
    t_rd = TT("rd")
    t_tmpa = TT("tmpa")
    t_sel = TT("sel")
    t_selb = TT("selb")
    t_selc = TT("selc")
    t_cmpw = TT("cmpw")
    t_blk = [TT(f"blk{i}") for i in range(4)]
    t_hid = [TT(f"hid{i}") for i in range(8)]
    t_kc = TT("kc")

    d_xs = [dsem("xs0"), dsem("xs1")]
    d_rt = [dsem("rt0"), dsem("rt1")]
    d_const = dsem("const")
    d_slab = dsem("slab")
    d_slabb = [dsem(f"slabb{i}") for i in range(5)]
    d_out = [dsem("out0"), dsem("out1")]
    d_selc = dsem("selc")

    t_const2 = TT("const2")
    d_const2 = dsem("const2")
    for dst, src in ((IDENT, ident_d), (GCOL, gcol_d)):
        dma(SP, d_const, dst[:], src, writes=[t_const])

    def late_consts():
        dma(SP, d_const2, TRAT01[:, 0:128], tri_d, writes=[t_const2])
        dma(SP, d_const2, TRAT01[:, 128:256], atri_d, writes=[t_const2])
        for dst, src in ((TRI, tri_d), (ATRI, atri_d), (CMASK, cmask_d), (ESEL, esel_d),
                         (KPOS, kposT_d), (VPOS, vposT_d), (ESINK, sinks_d)):
            dma(SP, d_const2, dst[:], src, writes=[t_const2])

    op(DVE, lambda e: e.memset(EPS_T[:], RMS_EPS), writes=[t_const])
    op(DVE, lambda e: e.memset(SS[:], 0.0), writes=[t_rstd])
    op(DVE, lambda e: e.memset(SS2[:], 0.0), writes=[t_rstd])
    op(DVE, lambda e: e.memset(SELB[:], 0.0), writes=[t_selb])
    op(DVE, lambda e: e.memset(VS[:, :, :, 128:129], 1.0), writes=[t_kv])
    op(DVE, lambda e: e.memset(VW[:, :, :, 128:129], 1.0), writes=[t_kv])
    op(DVE, lambda e: e.memset(VB[:, :, :, 64:65], 1.0), writes=[t_kv])

    def load_slab(dst, w_d, c0, ncols, blk0=0, dcol0=0, cb=512):
        wv = w_d.rearrange("(kc p) c -> p kc c", p=128)
        for j, cc in enumerate(range(0, ncols, cb)):
            w = min(cb, ncols - cc)
            for kc in range(0, 16, 8):
                dma(POOL, d_slabb[blk0 + j], dst[:, kc:kc + 8, dcol0 + cc:dcol0 + cc + w],
                    wv[:, kc:kc + 8, c0 + cc:c0 + cc + w], writes=[t_slabb[blk0 + j]])

    xt_count = [0]
    pm_count = [0]
    pending = []
    pre_bufs = {}

    def preissue_x(need_rt):
        xv0 = x_d.rearrange("(t p) d -> t p d", p=128)
        for tt in range(2):
            b = xt_count[0] % 2
            xt_count[0] += 1
            dma(SP, d_xs[b], XS[b][:], xv0[tt], writes=[t_xs[b]])
            if need_rt:
                dma(SP, d_rt[b], RT[b][:], ropetab_d[tt], writes=[t_rt[b]])
            pre_bufs[tt] = b

    def flush_pending():
        while pending:
            pending.pop(0)()

    def proj_pass(blocks, post, first=False, need_rt=False):
        xv = x_d.rearrange("(t p) d -> t p d", p=128)
        bufs = {}

        def issue_x(tt):
            b = xt_count[0] % 2
            xt_count[0] += 1
            dma(SP, d_xs[b], XS[b][:], xv[tt], writes=[t_xs[b]])
            bufs[tt] = b

        def issue_rt(tt):
            if need_rt:
                b = bufs[tt]
                dma(SP, d_rt[b], RT[b][:], ropetab_d[tt], writes=[t_rt[b]])

        def prep_act(tt):
            b = bufs[tt]
            if first:
                op(ACT, lambda e: e.activation(out=HB[:], in_=XS[b][:], func=ACTF.Square,
                                               accum_out=SS[:, tt:tt + 1]),
                   reads=[t_xs[b]], writes=[t_hb, t_rstd])
                op(ACT, lambda e: e.activation(out=SS[:, tt:tt + 1], in_=SS[:, tt:tt + 1], func=ACTF.Sqrt,
                                               scale=1.0 / D, bias=EPS_T[:, 0:1]),
                   reads=[t_const], writes=[t_rstd])
                op(DVE, lambda e: e.reciprocal(out=RSTD[:, tt:tt + 1], in_=SS[:, tt:tt + 1]),
                   writes=[t_rstd])
            op(ACT, lambda e: e.mul(out=HB[:], in_=XS[b][:], mul=RSTD[:, tt:tt + 1]),
               reads=[t_xs[b], t_rstd], writes=[t_hb])
            if tt + 2 < NT:
                issue_x(tt + 2)

        def prep_pe(tt):
            hb_i = tt % 2
            for half in range(2):
                bk = 4 + half
                grp(PE, [(lambda e, kc=kc: e.transpose(out=BKb[bk][:, (kc % 8) * 128:(kc % 8 + 1) * 128],
                                                        in_=HB[:, kc * 128:(kc + 1) * 128],
                                                        identity=IDENT[:]))
                         for kc in range(half * 8, half * 8 + 8)],
                    reads=[t_hb, t_const], writes=[t_bk[bk]])
                if half == 0 and not first:
                    grp(ACT, [(lambda e, kc=kc: e.mul(out=HT[hb_i][:, kc, :], in_=BKb[bk][:, kc * 128:(kc + 1) * 128],
                                                      mul=GCOL[:, kc:kc + 1])) for kc in range(8)],
                        reads=[t_bk[bk], t_const], writes=[t_ht[hb_i]])
                else:
                    op(DVE, lambda e: e.tensor_tensor(out=HT[hb_i][:, half * 8:half * 8 + 8, :],
                                                      in0=BKb[bk].rearrange("p (k t) -> p k t", k=8),
                                                      in1=bc_last(GCOL[:, half * 8:half * 8 + 8], 128), op=ALU.mult),
                       reads=[t_bk[bk], t_const], writes=[t_ht[hb_i]])

        if pre_bufs:
            bufs.update(pre_bufs)
            pre_bufs.clear()
        else:
            issue_x(0)
            issue_rt(0)
            issue_x(1)
            issue_rt(1)
        prep_act(0)
        prep_pe(0)
        nb = len(blocks)
        for tt in range(NT):
            b = bufs[tt]
            hb_i = tt % 2
            if tt + 1 < NT:
                prep_act(tt + 1)
            for bi, (slab, c0, w, sti) in enumerate(blocks):
                pi = pm_count[0] % 4
                pm_count[0] += 1
                pm = PM[pi]
                grp(PE, [(lambda e, kc=kc: e.matmul(pm[:, 0:w], lhsT=HT[hb_i][:, kc, :],
                                                    rhs=slab[:, kc, c0:c0 + w],
                                                    start=(kc == 0), stop=(kc == 15)))
                         for kc in range(16)],
                    reads=[t_ht[hb_i], t_slabb[sti]], writes=[t_pm[pi]])
                if w >= 256:
                    flush_pending()
                post(tt, bi, pm, t_pm[pi], w, b)
                if bi == max(0, nb - 3) and tt + 1 < NT:
                    prep_pe(tt + 1)
            if tt + 2 < NT:
                issue_rt(tt + 2)
            if first and tt == 4:
                late_consts()
        flush_pending()

    def rope(pm, t_pmi, c0, nh, d, rt_b, outs):
        hd = d // 2
        base = 0 if d == 128 else 256
        t_rtb = t_rt[rt_b]
        cos2 = RT[rt_b][:, base:base + d]
        nsin = RT[rt_b][:, base + d:base + d + hd]
        psin = RT[rt_b][:, base + d + hd:base + 2 * d]
        w = nh * d
        psv = pm[:, c0:c0 + w].rearrange("p (h d) -> p h d", h=nh)
        tav = TA[:, 0:w].rearrange("p (h d) -> p h d", h=nh)
        tbv = TB[:, 0:w].rearrange("p (h d) -> p h d", h=nh)
        op(DVE, lambda e: e.tensor_tensor(out=tav, in0=psv, in1=bc_mid(cos2, nh), op=ALU.mult),
           reads=[t_pmi, t_rtb], writes=[t_ta])
        op(DVE, lambda e: e.tensor_tensor(out=tbv[:, :, 0:hd], in0=psv[:, :, hd:d], in1=bc_mid(nsin, nh), op=ALU.mult),
           reads=[t_pmi, t_rtb], writes=[t_tb])
        op(DVE, lambda e: e.tensor_tensor(out=tbv[:, :, hd:d], in0=psv[:, :, 0:hd], in1=bc_mid(psin, nh), op=ALU.mult),
           reads=[t_pmi, t_rtb], writes=[t_tb])
        for (o, t_o) in outs:
            op(DVE, lambda e, o=o: e.tensor_tensor(out=o, in0=tav, in1=tbv, op=ALU.add),
               reads=[t_ta, t_tb], writes=[t_o])

    kr_cnt = [0, 0]

    def transposes(src_tile, t_src, nchunks, dsts, eng=None):
        def run():
            bk = 6 + kr_cnt[1] % 2
            kr_cnt[1] += 1
            grp(PE, [(lambda e, c=c: e.transpose(out=BKb[bk][:, c * 128:(c + 1) * 128],
                                                  in_=src_tile[:, c * 128:(c + 1) * 128], identity=IDENT[:]))
                     for c in range(nchunks)],
                reads=[t_src, t_const], writes=[t_bk[bk]])
            src = BKb[bk][:, 0:nchunks * 128].rearrange("p (k t) -> p k t", k=nchunks)
            for (dst_ap, lo, hi, tl) in dsts:
                op(ACT, lambda e, dst_ap=dst_ap, lo=lo, hi=hi: e.copy(out=dst_ap, in_=src[:, lo:hi, :]),
                   reads=[t_bk[bk]], writes=tl)
        kr_cnt[0] += 1
        pending.append(run)

    load_slab(QY, wkv_d, 0, 1792)
    t_w1k = TT("w1k")
    d_w1k = dsem("w1k")
    dma(POOL, d_w1k, W2K[:], ckw2_d.rearrange("(c j) d -> j c d", j=128), writes=[t_w1k])
    dma(POOL, d_w1k, W2V[:], cvw2_d.rearrange("(c j) d -> j c d", j=128), writes=[t_w1k])
    wvk = ckw1_d.rearrange("(l d) j -> d l j", d=128)
    W1KA = flat(R2[:, 14:16, :]).rearrange("p (l j) -> p l j", l=16)
    W1KB = SCR[:, 4096:7168].rearrange("p (l j) -> p l j", l=12)
    W1KC = flat(R2[:, 8:11, :])[:, 4128:5152].rearrange("p (l j) -> p l j", l=4)
    dma(POOL, d_w1k, W1KA[:, 0:8, :], wvk[:, 0:8, :], writes=[t_w1k])
    dma(POOL, d_w1k, W1KA[:, 8:16, :], wvk[:, 8:16, :], writes=[t_w1k])
    dma(POOL, d_w1k, W1KB[:, 0:8, :], wvk[:, 16:24, :], writes=[t_w1k])
    dma(POOL, d_w1k, W1KB[:, 8:12, :], wvk[:, 24:28, :], writes=[t_w1k])
    dma(POOL, d_w1k, W1KC, wvk[:, 28:32, :], writes=[t_w1k])
    W1K_L = [W1KA[:, l, :] for l in range(16)] + [W1KB[:, l, :] for l in range(12)] + [W1KC[:, l, :] for l in range(4)]

    def post1(tt, blk, pm, t_pmi, w, b):
        ts_ = slice(tt * 128, (tt + 1) * 128)
        ki = kr_cnt[0] % 4
        kr = KR[ki]
        if blk == 0:
            rope(pm, t_pmi, 0, 4, 128, b, [(kr[:, 0:512].rearrange("p (h d) -> p h d", h=4), t_kr[ki])])
            transposes(kr, t_kr[ki], 4, [(KT_A[:, 0:4, ts_], 0, 4, [t_kv])])
        elif blk == 1:
            rope(pm, t_pmi, 0, 2, 128, b, [(kr[:, 0:256].rearrange("p (h d) -> p h d", h=2), t_kr[ki])])
            kdup = kr[:, 256:512].rearrange("p (k c d) -> p k c d", k=2, c=2)
            rope(pm, t_pmi, 256, 2, 64, b, [(kdup[:, :, 0, :], t_kr[ki]), (kdup[:, :, 1, :], t_kr[ki])])
            op(ACT, lambda e: e.copy(out=VB[:, tt, :, 0:64], in_=pm[:, 384:512].rearrange("p (k d) -> p k d", k=2)),
               reads=[t_pmi], writes=[t_kv])
            transposes(kr, t_kr[ki], 4, [(KT_A[:, 4:6, ts_], 0, 2, [t_kv]), (KT_B[:, 0:2, ts_], 2, 4, [t_kv])])
        elif blk == 2:
            op(ACT, lambda e: e.copy(out=kr[:, 0:256], in_=pm[:, 0:256]), reads=[t_pmi], writes=[t_kr[ki]])
            op(ACT, lambda e: e.copy(out=VS[:, tt, :, 0:128], in_=pm[:, 256:512].rearrange("p (k d) -> p k d", k=2)),
               reads=[t_pmi], writes=[t_kv])
            transposes(kr, t_kr[ki], 2, [(VcaT[:, 0:2, ts_], 0, 2, [t_kv])])
        else:
            op(ACT, lambda e: e.copy(out=VW[:, tt, :, 0:128], in_=pm[:, 0:256].rearrange("p (k d) -> p k d", k=2)),
               reads=[t_pmi], writes=[t_kv])

    proj_pass([(QY, 0, 512, 0), (QY, 512, 512, 1), (QY, 1024, 512, 2), (QY, 1536, 256, 3)], post1, first=True, need_rt=True)
    barrier()
    W1V = [XS[0][:].bitcast(BF16).rearrange("p (l j) -> p l j", l=16),
           XS[1][:].bitcast(BF16).rearrange("p (l j) -> p l j", l=16)]
    wvv = cvw1_d.rearrange("(l d) j -> d l j", d=128)
    for hf in range(2):
        for l0 in range(0, 16, 8):
            dma(POOL, d_slabb[4], W1V[hf][:, l0:l0 + 8, :], wvv[:, hf * 16 + l0:hf * 16 + l0 + 8, :],
                writes=[t_slabb[4]])
    W1L = [W1K_L, [W1V[l // 16][:, l % 16, :] for l in range(32)]]
    t_w1 = [t_w1k, t_slabb[4]]
    SL2 = flat(QY[:, 8:16, :]).rearrange("p (k c) -> p k c", k=16)
    load_slab(SL2, wq1_d, 0, 1024)
    load_slab(GAW, wq1_d, 1024, 24, blk0=2)

    op(DVE, lambda e: e.memset(VCX[:], 0.0), writes=[t_kc])
    op(DVE, lambda e: e.memset(KCT[:], 0.0), writes=[t_kc])
    op(DVE, lambda e: e.memset(VCX[:, :, 128:129], 1.0), writes=[t_kc])
    for k in range(2):
        dma(SP, d_const, VCX[:, k, 129:161], overlap_d, writes=[t_kc])
    BLK = [flat(QY[:, 2 * x:2 * x + 2, :])[:, 0:4064].rearrange("p (l c) -> p l c", l=32) for x in range(4)]
    HID = [SCR[:, i * 128:i * 128 + 127] for i in range(8)]
    for x, (src, h, pos) in enumerate(((KT_A, 0, KPOS), (KT_A, 1, KPOS), (VcaT, 0, VPOS), (VcaT, 1, VPOS))):
        s2 = src[:, h, :]
        a = s2.ap
        srcv = bass.AP(s2.tensor, s2.offset, [list(a[0]), [1, 32], [16, 127]])
        if x % 2 == 0:
            op(DVE, lambda e, x=x, srcv=srcv, pos=pos: e.tensor_tensor(out=BLK[x], in0=srcv, in1=bc_last(pos[:, 0:32], 127), op=ALU.add),
               reads=[t_const], writes=[t_blk[x]])
        else:
            grp(ACT, [(lambda e, x=x, l=l, s2=s2, a=a, pos=pos: e.activation(
                out=BLK[x][:, l, :], in_=bass.AP(s2.tensor, s2.offset + l, [list(a[0]), [16, 127]]),
                func=ACTF.Identity, bias=pos[:, l:l + 1])) for l in range(32)],
                reads=[t_const], writes=[t_blk[x]])
    for x in range(4):
        kv = x // 2
        for jc in range(2):
            pi = pm_count[0] % 4
            pm_count[0] += 1
            grp(PE, [(lambda e, l=l: e.matmul(PM[pi][:, 0:127], lhsT=W1L[kv][l][:, jc * 128:(jc + 1) * 128],
                                              rhs=BLK[x][:, l, :], start=(l == 0), stop=(l == 31)))
                     for l in range(32)],
                reads=[t_w1[kv], t_blk[x]], writes=[t_pm[pi]])
            op(ACT, lambda e: e.activation(out=HID[x * 2 + jc], in_=PM[pi][:, 0:127], func=ACTF.Silu),
               reads=[t_pm[pi]], writes=[t_hid[x * 2 + jc]])
    for x in range(4):
        h = x % 2
        pi = pm_count[0] % 4
        pm_count[0] += 1
        if x < 2:
            grp(PE, [(lambda e, jc=jc: e.matmul(PM[pi][:, 0:127], lhsT=W2K[:, jc, :], rhs=HID[x * 2 + jc],
                                                start=(jc == 0), stop=(jc == 1))) for jc in range(2)],
                reads=[t_w1k, t_hid[x * 2], t_hid[x * 2 + 1]], writes=[t_pm[pi]])
            op(ACT, lambda e: e.copy(out=KCT[:, h, 0:127], in_=PM[pi][:, 0:127]), reads=[t_pm[pi]], writes=[t_kc])
        else:
            grp(PE, [(lambda e, jc=jc: e.matmul(PM[pi][0:127, 0:128], lhsT=HID[x * 2 + jc], rhs=W2V[:, jc, :],
                                                start=(jc == 0), stop=(jc == 1))) for jc in range(2)],
                reads=[t_w1k, t_hid[x * 2], t_hid[x * 2 + 1]], writes=[t_pm[pi]])
            op(ACT, lambda e: e.copy(out=VCX[0:127, h, 0:128], in_=PM[pi][0:127, 0:128]), reads=[t_pm[pi]], writes=[t_kc])
    barrier()

    if "kv" in debug:
        dd = dout("dbg_KCT", [128, 2, 128], BF16)
        dma(SP, d_out[0], dd, KCT[:], reads=[t_kc])
        dd = dout("dbg_VCX", [128, 2, 168], BF16)
        dma(SP, d_out[0], dd, VCX[:], reads=[t_kc])
        barrier()


    def post2(tt, blk, pm, t_pmi, w, b):
        ts_ = slice(tt * 128, (tt + 1) * 128)
        Q = tt // 4
        if blk < 2:
            ki = kr_cnt[0] % 4
            kr = KR[ki]
            rope(pm, t_pmi, 0, 4, 128, b, [(kr[:, 0:512].rearrange("p (h d) -> p h d", h=4), t_kr[ki])])
            transposes(kr, t_kr[ki], 4, [(QY[:, blk * 4:blk * 4 + 4, ts_], 0, 4, [t_q[blk * 4 + j][Q] for j in range(4)])])
        else:
            op(ACT, lambda e: e.activation(out=GATES[:, tt, :], in_=pm[:, 0:24], func=ACTF.Sigmoid),
               reads=[t_pmi], writes=[t_gates])

    proj_pass([(SL2, 0, 512, 0), (SL2, 512, 512, 1), (GAW, 0, 24, 2)], post2, need_rt=True)
    barrier()

    SCB = [0, 1, 2, 7]
    PVB = [(3, 4), (5, 6)]
    TRB = 7
    LA = 4
    st_cnt = [0]
    un_cnt = [0]

    class Unit:
        pass

    def run_units(units):
        flat = []
        for u in units:
            u.pair = un_cnt[0] % 2
            un_cnt[0] += 1
            u.first = [True, True]
            for si in range(len(u.steps)):
                flat.append((u, si))
        n = len(flat)
        slots = {}
        for idx in range(n + LA):
            if idx < n:
                u, si = flat[idx]
                g = st_cnt[0]
                st_cnt[0] += 1
                slots[idx] = g
                if si == 0 and getattr(u, "pre", None) is not None:
                    u.pre()
                u.emit_s(u, u.steps[si], g, g % 5)
            j = idx - LA
            if j >= 0:
                u, si = flat[j]
                g = slots.pop(j)
                u.emit_pv(u, u.steps[si], g % 5)
                if si == len(u.steps) - 1:
                    u.finish(u)

    inv_sqrt_a = 1.0 / math.sqrt(128.0)

    def a_emit_s(u, st, gstep, xi):
        segs, masks = st
        sb_ = SCB[gstep % 4]
        pmS = BK[sb_]
        kr_ = u.krows
        LO = min(sg[1] for sg in segs)
        HI = max(sg[2] for sg in segs)
        fns = []
        for (kt, lo, hi) in segs:
            fns.append(lambda e, kt=kt, lo=lo, hi=hi: e.matmul(pmS[0:kr_, lo:hi], lhsT=u.lhsT(kt), rhs=u.q(lo, hi),
                                                             start=True, stop=(u.bias is None)))
            if u.bias is not None:
                bl, br_ = u.bias(kt, lo, hi)
                fns.append(lambda e, lo=lo, hi=hi, bl=bl, br_=br_: e.matmul(pmS[0:kr_, lo:hi], lhsT=bl, rhs=br_,
                                                                           start=False, stop=True))
        grp(PE, fns, reads=[u.tq, t_selb, t_const], writes=[t_bk[sb_]])
        px = PX[xi]
        op(ACT, lambda e: e.activation(out=px[0:kr_, LO:HI], in_=pmS[0:kr_, LO:HI], func=ACTF.Exp, scale=u.scale),
           reads=[t_bk[sb_]], writes=[t_px[xi]])
        for (i, m) in masks:
            mk = TRI if m == "tri" else ATRI
            ME = POOL
            op(ME, lambda e, i=i, mk=mk: e.tensor_tensor(out=px[:, i * 128:(i + 1) * 128],
                                                         in0=px[:, i * 128:(i + 1) * 128],
                                                         in1=mk[:], op=ALU.mult),
               reads=[t_const], writes=[t_px[xi]])

    def a_emit_pv(u, st, xi):
        segs, masks = st
        px = PX[xi]
        banks = [BK[PVB[u.pair][0]], BK[PVB[u.pair][1]]]
        tb = [t_bk[PVB[u.pair][0]], t_bk[PVB[u.pair][1]]]
        fns = []
        used = set()
        for (kt, lo, hi) in segs:
            vv = u.v(kt)
            vw = vv.shape[-1]
            for i in range(lo // 128, hi // 128):
                bk = i // 2
                stf = u.first[bk]
                u.first[bk] = False
                used.add(bk)
                fns.append(lambda e, i=i, bk=bk, stf=stf, vv=vv, vw=vw: e.matmul(
                    banks[bk][:, (i % 2) * 256:(i % 2) * 256 + vw], lhsT=px[0:u.krows, i * 128:(i + 1) * 128],
                    rhs=vv, start=stf, stop=False, skip_group_check=True))
        grp(PE, fns, reads=[t_px[xi], t_kv, t_kc], writes=[tb[bk] for bk in sorted(used)])

    def a_finish(u):
        pr = u.pair
        tb = [t_bk[PVB[pr][0]], t_bk[PVB[pr][1]]]
        h, Q, br, g = u.h, u.Q, u.br, u.g
        pv4 = PVT[pr][:].rearrange("p (t c) -> p t c", t=4)
        if br == 0:
            op(DVE, lambda e: e.tensor_scalar(out=RD[:, 0:4], in0=pv4[:, :, 128], scalar1=TINY, scalar2=None, op0=ALU.max),
               reads=tb, writes=[t_rd])
            op(DVE, lambda e: e.reciprocal(out=RD2[:, 0:4], in_=RD[:, 0:4]), writes=[t_rd])
        else:
            op(DVE, lambda e: e.reciprocal(out=RD2[:, 0:4], in_=pv4[:, :, 128]), reads=tb, writes=[t_rd])
        op(DVE, lambda e: e.tensor_tensor(out=GW[:], in0=RD2[:, 0:4], in1=GATES[:, 4 * Q:4 * Q + 4, h * 3 + br], op=ALU.mult),
           reads=[t_gates], writes=[t_rd])
        if u.first_branch:
            op(DVE, lambda e: e.tensor_tensor(out=ACC[g], in0=pv4[:, :, 0:128], in1=bc_last(GW[:], 128), op=ALU.mult),
               reads=tb + [t_rd], writes=[t_acc[g]])
        else:
            op(DVE, lambda e: e.tensor_tensor(out=TMPA, in0=pv4[:, :, 0:128], in1=bc_last(GW[:], 128), op=ALU.mult),
               reads=tb + [t_rd], writes=[t_tmpa])
            if br == 1:
                op(DVE, lambda e: e.tensor_tensor(out=QY[:, h, Q * 512:(Q + 1) * 512].rearrange("p (t d) -> p t d", t=4),
                                                  in0=TMPA, in1=ACC[g], op=ALU.add),
                   reads=[t_tmpa, t_acc[g]], writes=[u.tq])
            else:
                op(DVE, lambda e: e.tensor_tensor(out=ACC[g], in0=TMPA, in1=ACC[g], op=ALU.add),
                   reads=[t_tmpa], writes=[t_acc[g]])
        if u.want_psel:
            if g == 0:
                op(DVE, lambda e: e.tensor_tensor(out=PSEL[:], in0=pv4[:, :, 129:161], in1=bc_last(RD2[:, 0:4], 32), op=ALU.mult),
                   reads=tb + [t_rd], writes=[t_sel])
            else:
                op(DVE, lambda e: e.tensor_tensor(out=TMPS[:], in0=pv4[:, :, 129:161], in1=bc_last(RD2[:, 0:4], 32), op=ALU.mult),
                   reads=tb + [t_rd], writes=[t_tmpa])
                op(DVE, lambda e: e.tensor_tensor(out=PSEL[:], in0=PSEL[:], in1=TMPS[:], op=ALU.add),
                   reads=[t_tmpa], writes=[t_sel])
        if u.after is not None:
            u.after()

    def selection(k, Q):
        op(DVE, lambda e: e.tensor_tensor(out=SCO[:], in0=PSEL[:], in1=SELCQ[:, :, 0:32], op=ALU.mult),
           reads=[t_selc], writes=[t_sel])
        op(DVE, lambda e: e.tensor_tensor(out=SCO[:], in0=SCO[:], in1=SELCQ[:, :, 32:64], op=ALU.add),
           reads=[t_selc], writes=[t_sel])
        for i in range(4):
            op(DVE, lambda e: e.max(out=M8[:], in_=SCO[:, i, :]), writes=[t_sel])
            op(DVE, lambda e: e.match_replace(out=SC2[:], in_to_replace=M8[:], in_values=SCO[:, i, :],
                                              imm_value=-2.0), writes=[t_sel])
            op(DVE, lambda e: e.max(out=M8b[:], in_=SC2[:]), writes=[t_sel])
            op(DVE, lambda e: e.tensor_scalar(out=SC2[:], in0=SCO[:, i, :], scalar1=M8b[:, 7:8], scalar2=None,
                                              op0=ALU.is_lt), writes=[t_sel])
            op(DVE, lambda e: e.tensor_scalar(out=NB[:, i, :], in0=SC2[:], scalar1=NEGBIG, scalar2=None,
                                              op0=ALU.mult), writes=[t_sel])
        grp(PE, [(lambda e, i=i: e.transpose(out=BKb[TRB][0:32, i * 128:(i + 1) * 128], in_=NB[:, i, :],
                                              identity=IDENT[:])) for i in range(4)],
            reads=[t_sel, t_const], writes=[t_bk[TRB]])
        op(ACT, lambda e: e.copy(out=SELB[0:32, k, (Q - 2) * 512:(Q - 1) * 512], in_=BKb[TRB][0:32, 0:512]),
           reads=[t_bk[TRB]], writes=[t_selb])

    def mk_a_unit(k, Q, g, br):
        u = Unit()
        h = 4 * k + g
        u.h, u.Q, u.br, u.g, u.k = h, Q, br, g, k
        u.tq = t_q[h][Q]
        u.q = lambda lo, hi: QY[:, h, Q * 512 + lo:Q * 512 + hi]
        u.scale = inv_sqrt_a
        u.bias = None
        u.krows = 128
        u.after = None
        u.want_psel = False
        u.first_branch = False
        u.emit_s, u.emit_pv, u.finish = a_emit_s, a_emit_pv, a_finish
        if br == 0:
            u.steps = [([(0, 0, 512)], [])]
            u.bias = lambda kt, lo, hi: (IDENT[:, 0:127], CMASK[:, Q * 512 + lo:Q * 512 + hi])
            u.lhsT = lambda kt: KCT[:, k, 0:127]
            u.v = lambda kt: VCX[0:127, k, 0:161]
            u.krows = 127
            u.first_branch = True
            u.want_psel = Q >= 2
            if g == 3 and Q >= 2:
                u.after = lambda: selection(k, Q)
        elif br == 2:
            steps = []
            if Q == 0:
                for r in range(0, 4):
                    steps.append(([(r, r * 128, 512)], [(r, "tri")]))
            else:
                kt0 = 4 * Q
                for j in range(3):
                    steps.append(([(kt0 - 4 + j, 0, (j + 1) * 128), (kt0 + 1 + j, (j + 1) * 128, 512)],
                                  [(j, "atri"), (j + 1, "tri")]))
                steps.append(([(kt0 - 1, 0, 512)], [(3, "atri")]))
                steps.append(([(kt0, 0, 512)], [(0, "tri")]))
            u.steps = steps
            u.lhsT = lambda kt: KT_A[:, 4 + k, kt * 128:(kt + 1) * 128]
            u.v = lambda kt: VW[:, kt, k, 0:129]
        else:
            steps = []
            for kt in range(0, 4 * Q + 4):
                r = kt - 4 * Q
                if r < 0:
                    steps.append(([(kt, 0, 512)], []))
                else:
                    steps.append(([(kt, r * 128, 512)], [(r, "tri")]))
            u.steps = steps
            u.lhsT = lambda kt: KT_A[:, 2 + k, kt * 128:(kt + 1) * 128]
            u.v = lambda kt: VS[:, kt, k, 0:129]
            if Q >= 2:
                u.bias = lambda kt, lo, hi: (ESEL[:, kt * 128:(kt + 1) * 128],
                                             SELB[:, k, (Q - 2) * 512 + lo:(Q - 2) * 512 + hi])
        return u

    groups = [(k, Q) for k in range(2) for Q in range(NQG)]

    def cmp_unit(n, g):
        k, Q = groups[n]
        u = mk_a_unit(k, Q, g, 0)
        if g == 0 and Q >= 2:
            u.pre = lambda: dma(SP, d_selc, SELCQ[:], selc_d[4 * Q:4 * Q + 4].rearrange("t p c -> p t c"),
                                writes=[t_selc])
        return u

    wv_za = wq2_d.rearrange("(kc p) c -> p kc c", p=128)

    def za_piece(j, kc):
        return lambda: dma(POOL, d_slabb[j], SL2[:, kc:kc + 8, j * 512:(j + 1) * 512],
                           wv_za[:, kc:kc + 8, j * 512:(j + 1) * 512], writes=[t_slabb[j]])
    za_pieces = [za_piece(j, kc) for j in range(2) for kc in (0, 8)]
    units = [cmp_unit(0, g) for g in range(4)]
    for n, (k, Q) in enumerate(groups):
        if Q >= 2:
            for g in range(4):
                units.append(mk_a_unit(k, Q, g, 2))
        for g in range(4):
            if Q < 2:
                units.append(mk_a_unit(k, Q, g, 2))
            usel = mk_a_unit(k, Q, g, 1)
            if k == 0 and Q == 2:
                usel.pre = za_pieces[g]
            units.append(usel)
            if n + 1 < len(groups):
                units.append(cmp_unit(n + 1, g))
    run_units(units)
    preissue_x(True)
    barrier()

    if "oa" in debug:
        dd = dout("dbg_OA", [128, 8, S], BF16)
        dma(SP, d_out[0], dd, QY[:, 0:8, :])
        dd = dout("dbg_GATES", [128, NT, 24], F32)
        dma(SP, d_out[0], dd, GATES[:])
        barrier()

    def post_z(cbase):
        def f(tt, blk, pm, t_pmi, w, b):
            ts_ = slice(tt * 128, (tt + 1) * 128)
            Q = tt // 4
            cb = cbase + blk * 4
            tl = [t_q[cb + j][Q] for j in range(4)]
            op(ACT, lambda e: e.activation(out=TA[:, 0:512], in_=pm[:, 0:512], func=ACTF.Silu),
               reads=[t_pmi], writes=[t_ta])
            ki = kr_cnt[0] % 4
            kr = KR[ki]
            op(DVE, lambda e: e.tensor_tensor(out=kr[:, 0:512].rearrange("p (c d) -> p c d", c=4),
                                              in0=TA[:, 0:512].rearrange("p (c d) -> p c d", c=4),
                                              in1=QY[:, cb:cb + 4, ts_], op=ALU.mult),
               reads=[t_ta] + tl, writes=[t_kr[ki]])
            transposes(kr, t_kr[ki], 4, [(QY[:, cb:cb + 4, ts_], 0, 4, tl)])
        return f

    load_slab(RLO, wq2_d, 1024, 1024, blk0=2)
    pz3 = post_z(0)

    def post3(tt, blk, pm, t_pmi, w, b):
        if blk < 2:
            pz3(tt, blk, pm, t_pmi, w, b)
        else:
            ts_ = slice(tt * 128, (tt + 1) * 128)
            Q = tt // 4
            ki = kr_cnt[0] % 4
            kr = KR[ki]
            cb = 8 + (blk - 2) * 4
            rope(pm, t_pmi, 0, 8, 64, b, [(kr[:, 0:512].rearrange("p (h d) -> p h d", h=8), t_kr[ki])])
            transposes(kr, t_kr[ki], 4, [(R2[:, cb:cb + 4, ts_], 0, 4, [t_q[cb + j][Q] for j in range(4)])])

    proj_pass([(SL2, 0, 512, 0), (SL2, 512, 512, 1), (RLO, 0, 512, 2), (RLO, 512, 512, 3)], post3, need_rt=True)
    barrier()

    load_slab(RLO, wz2_d, 0, 1024)
    op(ACT, lambda e: e.activation(out=ESINK[:], in_=ESINK[:], func=ACTF.Exp), writes=[t_const])

    KBZ = [XS[0][:].bitcast(BF16).rearrange("p (k t) -> p k t", k=2), XS[1][:].bitcast(BF16).rearrange("p (k t) -> p k t", k=2)]
    op(DVE, lambda e: e.memset(KBZ[0][64:128, :, :], 0.0), writes=[t_xs[0]])
    op(DVE, lambda e: e.memset(KBZ[1][0:64, :, :], 0.0), writes=[t_xs[1]])
    op(DVE, lambda e: e.tensor_copy(out=KBZ[0][0:64, :, :], in_=KT_B[0:64, :, :]), reads=[t_kv], writes=[t_xs[0]])
    op(ACT, lambda e: e.copy(out=KBZ[1][64:128, :, :], in_=KT_B[64:128, :, :]), reads=[t_kv], writes=[t_xs[1]])

    def b_emit_s(u, st, gstep, xi):
        kt, lo, hi, mk = st
        w = hi - lo
        sbs = [(0, 1), (2, 7)][gstep % 2]
        px = PX[xi]
        pxv = px[:, 0:512].rearrange("p (e c) -> p e c", e=2)[:, :, 0:w]
        fns = []
        for e_ in range(2):
            fns.append(lambda e, e_=e_: e.matmul(BK[sbs[e_]][:, 0:w], lhsT=KBZ[e_][:, u.kvb, kt * 128:(kt + 1) * 128],
                                                 rhs=R2[:, 8 + u.i, u.Q * 512 + lo:u.Q * 512 + hi],
                                                 start=True, stop=not B_PE_BIAS))
        if B_PE_BIAS:
            for e_ in range(2):
                fns.append(lambda e, e_=e_: e.matmul(BK[sbs[e_]][:, 0:w], lhsT=IDENT[:], rhs=mk, start=False, stop=True))
        grp(PE, fns, reads=[u.tq, t_xs[0], t_xs[1], t_const], writes=[t_bk[sbs[0]], t_bk[sbs[1]]])
        for e_ in range(2):
            pmS = BK[sbs[e_]]
            op(ACT, lambda e: e.activation(out=px[:, e_ * 256:e_ * 256 + w], in_=pmS[:, 0:w],
                                           func=ACTF.Exp, scale=0.125),
               reads=[t_bk[sbs[e_]]], writes=[t_px[xi]])
        if not B_PE_BIAS:
            pxv = px[:, 0:512].rearrange("p (e c) -> p e c", e=2)[:, :, 0:w]
            op(DVE, lambda e: e.tensor_tensor(out=pxv, in0=pxv, in1=bc_mid(u.mk01(st), 2), op=ALU.mult),
               reads=[t_const], writes=[t_px[xi]])

    def b_emit_pv(u, st, xi):
        kt, lo, hi, mk = st
        px = PX[xi]
        fns = []
        for e_ in range(2):
            bank = BK[PVB[u.pair][e_]]
            for j in range((hi - lo) // 128):
                i = lo // 128 + j
                stf = u.first[e_]
                u.first[e_] = False
                fns.append(lambda e, e_=e_, j=j, i=i, stf=stf, bank=bank: e.matmul(
                    bank[:, i * 128:i * 128 + 65], lhsT=px[:, e_ * 256 + j * 128:e_ * 256 + (j + 1) * 128],
                    rhs=VB[:, kt, u.kvb, 0:65], start=stf, stop=False, skip_group_check=True))
        grp(PE, fns, reads=[t_px[xi], t_kv], writes=[t_bk[PVB[u.pair][0]], t_bk[PVB[u.pair][1]]])

    def b_finish(u):
        pr = u.pair
        tb = [t_bk[PVB[pr][0]], t_bk[PVB[pr][1]]]
        pvb = PVT[pr][:].rearrange("p (e t c) -> p e t c", e=2, t=4)
        op(DVE, lambda e: e.tensor_tensor(out=RD[:].rearrange("p (e t) -> p e t", e=2), in0=pvb[:, :, :, 64],
                                          in1=bc_last(ESINK[:, 2 * u.i:2 * u.i + 2], 4), op=ALU.add),
           reads=tb + [t_const], writes=[t_rd])
        op(DVE, lambda e: e.reciprocal(out=RD2[:], in_=RD[:]), writes=[t_rd])
        qc = slice(u.Q * 512, (u.Q + 1) * 512)
        op(DVE, lambda e: e.tensor_tensor(out=QY[:, 8 + u.i, qc].rearrange("p (t e d) -> p e t d", t=4, e=2),
                                          in0=pvb[:, :, :, 0:64],
                                          in1=bc_last(RD2[:].rearrange("p (e t) -> p e t", e=2), 64), op=ALU.mult),
           reads=tb + [t_rd], writes=[u.tq])

    bunits = []
    for i in range(8):
        for Q in range(NQG):
            u = Unit()
            u.i, u.Q, u.kvb = i, Q, i // 4
            u.tq = t_q[8 + i][Q]
            steps = []
            for r in range(-1, 4):
                kt = 4 * Q + r
                if kt < 0:
                    continue
                if r < 0:
                    steps.append((kt, 0, 128, TRAT[:, 128:256]))
                elif r < 3:
                    steps.append((kt, r * 128, (r + 2) * 128, TRAT[:]))
                else:
                    steps.append((kt, 384, 512, TRAT[:, 0:128]))
            u.steps = steps
            u.mk01 = lambda st: (ATRI[:] if st[2] - st[1] == 128 and st[1] == 0 else (TRI[:] if st[2] - st[1] == 128 else TRAT01[:]))
            u.emit_s, u.emit_pv, u.finish = b_emit_s, b_emit_pv, b_finish
            bunits.append(u)
    run_units(bunits)
    barrier()

    if "ob" in debug:
        dd = dout("dbg_OB", [128, 8, S], BF16)
        dma(SP, d_out[0], dd, QY[:, 8:16, :])
        barrier()

    load_slab(RHI, wout_d, 1024, 1024, blk0=2)
    proj_pass([(RLO, 0, 512, 0), (RLO, 512, 512, 1)], post_z(8))
    barrier()

    load_slab(RLO, wout_d, 0, 1024)
    FG = flat(KT_B[:]).bitcast(F32)
    dma(SP, d_const, FG, fg_d, writes=[t_const])
    xv = x_d.rearrange("(t p) d -> t p d", p=128)
    ov = out_d.rearrange("(t p) d -> t p d", p=128)
    def epilogue(tt, b):
        def run():
            op(ACT, lambda e: e.activation(out=HB[:], in_=XS[b][:], func=ACTF.Square, accum_out=SS2[:, tt:tt + 1]),
               reads=[t_xs[b]], writes=[t_hb, t_rstd])
            op(ACT, lambda e: e.activation(out=SS2[:, tt:tt + 1], in_=SS2[:, tt:tt + 1], func=ACTF.Sqrt,
                                           scale=1.0 / D, bias=EPS_T[:, 0:1]), writes=[t_rstd])
            op(DVE, lambda e: e.reciprocal(out=RS2[:, tt:tt + 1], in_=SS2[:, tt:tt + 1]), writes=[t_rstd])
            op(DVE, lambda e: e.scalar_tensor_tensor(out=XS[b][:], in0=XS[b][:], scalar=RS2[:, tt:tt + 1], in1=FG,
                                                     op0=ALU.mult, op1=ALU.mult),
               reads=[t_rstd, t_const], writes=[t_xs[b]])
            dma(SP, d_out[b], ov[tt], XS[b][:], reads=[t_xs[b]])
        return run

    p5_cnt = 0
    sched = [(0, 2), (0, 3), (1, 2), (1, 3), (0, 0), (0, 1), (1, 0), (1, 1)]
    for tt in range(2, NT):
        sched += [(tt, 2), (tt, 3), (tt, 0), (tt, 1)]
    done_blocks = {}
    loaded = set()
    epi_q = []
    for (tt, blk) in sched:
        b = tt % 2
        ts_ = slice(tt * 128, (tt + 1) * 128)
        if tt not in loaded:
            loaded.add(tt)
            dma(SP, d_xs[b], XS[b][:], xv[tt], writes=[t_xs[b]])
        pi = p5_cnt % 8
        p5_cnt += 1
        grp(PE, [(lambda e, c=c: e.matmul(BK[pi][:, 0:512], lhsT=QY[:, c, ts_],
                                          rhs=(RLO if blk < 2 else RHI)[:, c, (blk % 2) * 512:(blk % 2 + 1) * 512],
                                          start=(c == 0), stop=(c == 15))) for c in range(16)],
            reads=[t_slabb[blk]], writes=[t_bk[pi]])
        op(DVE, lambda e: e.tensor_tensor(out=XS[b][:, blk * 512:(blk + 1) * 512], in0=BK[pi][:, 0:512],
                                          in1=XS[b][:, blk * 512:(blk + 1) * 512], op=ALU.add),
           reads=[t_bk[pi]], writes=[t_xs[b]])
        done_blocks[tt] = done_blocks.get(tt, 0) + 1
        for item in epi_q:
            item[1] -= 1
        while epi_q and epi_q[0][1] <= 0:
            epi_q.pop(0)[0]()
        if done_blocks[tt] == 4:
            epi_q.append([epilogue(tt, b), 2])
    for item in epi_q:
        item[0]()

    for d in (d_out[0], d_out[1]):
        if d.n > 0:
            SP.wait(Ev(d, d.n))
    return nc, ctx


def make_in_maps(x, w_in, cmp_k_w1, cmp_k_w2, cmp_v_w1, cmp_v_w2, cmp_k_pos, cmp_v_pos,
                 sinks, w_out, norm_g, final_g, n_cores=8):
    c = _consts()
    w_kv, w_q1, w_q2, w_z2 = _split_w_in(np.asarray(w_in[0], np.float32))
    shared = {
        "w_kv": w_kv, "w_q1": w_q1, "w_q2": w_q2, "w_z2": w_z2,
        "w_out": np.ascontiguousarray(w_out[0], np.float32),
        "cmp_k_w1": np.ascontiguousarray(cmp_k_w1[0], np.float32),
        "cmp_k_w2": np.ascontiguousarray(cmp_k_w2[0], np.float32),
        "cmp_v_w1": np.ascontiguousarray(cmp_v_w1[0], np.float32),
        "cmp_v_w2": np.ascontiguousarray(cmp_v_w2[0], np.float32),
        "kposT": np.ascontiguousarray(np.asarray(cmp_k_pos[0], np.float32).T),
        "vposT": np.ascontiguousarray(np.asarray(cmp_v_pos[0], np.float32).T),
        "sinks_bc": np.ascontiguousarray(np.broadcast_to(np.asarray(sinks[0], np.float32)[None, :], (128, 16))),
        "g_col": np.ascontiguousarray(np.asarray(norm_g[0], np.float32).reshape(16, 128).T),
        "fg_bc": np.ascontiguousarray(np.broadcast_to(np.asarray(final_g, np.float32)[None, :], (128, D))),
    }
    shared.update(c)
    in_maps = []
    for b in range(n_cores):
        m = dict(shared)
        m["x"] = np.ascontiguousarray(x[b], np.float32)
        in_maps.append(m)
    return in_maps


def kernel(**inputs):
    inputs = {k: np.asarray(v) for k, v in inputs.items()}
    in_maps = make_in_maps(**inputs)
    nc, ctx = build()
    with ctx:
        res = run_bass_kernel_spmd(nc, in_maps, core_ids=list(range(8)))
    outs = [np.asarray(r["out"], np.float32) for r in res.results]
    return np.stack(outs, axis=0)
```

```python
import math
from contextlib import ExitStack

import numpy as np
import ml_dtypes

import concourse.bass as bass
import concourse.mybir as mybir
from concourse.bass_utils import run_bass_kernel_spmd

F32 = mybir.dt.float32
BF16 = mybir.dt.bfloat16
ALU = mybir.AluOpType
ACTF = mybir.ActivationFunctionType
AX = mybir.AxisListType

S = 2048
D = 2048
NT = 16
NQG = 4
RMS_EPS = 1e-6
TINY = 1e-30
NEGBIG = -30000.0
B_PE_BIAS = False


class DmaSem:
    def __init__(self, sem):
        self.sem = sem
        self.n = 0


class Ev:
    __slots__ = ("holder", "val")

    def __init__(self, holder, val):
        self.holder = holder
        self.val = val


class Eng:
    def __init__(self, nc, ctx, eng, name):
        self.nc = nc
        self.e = eng
        self.name = name
        self.sem = ctx.enter_context(nc.semaphore("sem_" + name))
        self.n = 0
        self.waited = {}

    def wait(self, ev):
        if ev is None:
            return
        h = ev.holder
        val = h.n if isinstance(h, DmaSem) else ev.val
        key = id(h)
        if self.waited.get(key, 0) >= val:
            return
        self.waited[key] = val
        self.e.wait_ge(h.sem, val)

    def done(self, inst):
        self.n += 1
        inst.then_inc(self.sem, 1)
        return Ev(self, self.n)


class T:
    def __init__(self, name="", excl=False):
        self.name = name
        self.w = None
        self.r = []
        self.excl = excl


def _split(reads, writes):
    ex = [t for t in reads if t.excl]
    if ex:
        reads = [t for t in reads if not t.excl]
        writes = list(writes) + [t for t in ex if t not in writes]
    return reads, writes


def _pre(E, reads, writes):
    for t in reads:
        E.wait(t.w)
    for t in writes:
        E.wait(t.w)
        for r in t.r:
            E.wait(r)


def _post(ev, reads, writes):
    for t in reads:
        t.r.append(ev)
    for t in writes:
        t.w = ev
        t.r = []


def op(E, fn, reads=(), writes=()):
    reads, writes = _split(reads, writes)
    _pre(E, reads, writes)
    inst = fn(E.e)
    ev = E.done(inst)
    _post(ev, reads, writes)
    return ev


def grp(E, fns, reads=(), writes=()):
    reads, writes = _split(reads, writes)
    _pre(E, reads, writes)
    inst = None
    for fn in fns:
        inst = fn(E.e)
    ev = E.done(inst)
    _post(ev, reads, writes)
    return ev


def dma(E, dsem, out, in_, reads=(), writes=(), **kw):
    _pre(E, reads, writes)
    E.e.dma_start(out=out, in_=in_, **kw).then_inc(dsem.sem, 16)
    dsem.n += 16
    ev = Ev(dsem, dsem.n)
    _post(ev, reads, writes)
    return ev


def bc_mid(ap2d, n):
    a = ap2d.ap
    return bass.AP(ap2d.tensor, ap2d.offset, [list(a[0]), [0, n]] + [list(x) for x in a[1:]])


def bc_last(ap2d, n):
    a = ap2d.ap
    return bass.AP(ap2d.tensor, ap2d.offset, [list(x) for x in a] + [[0, n]])


def _consts():
    bf = ml_dtypes.bfloat16
    c = {}
    pos = np.arange(S, dtype=np.float32)

    def tab(d):
        inv = (np.float32(10000.0) ** (-(np.arange(0, d, 2, dtype=np.float32)) / np.float32(d))).astype(np.float32)
        ang = (pos[:, None] * inv[None, :]).astype(np.float32)
        return np.cos(ang).astype(np.float32), np.sin(ang).astype(np.float32)

    cA, sA = tab(128)
    cB, sB = tab(64)
    rt = np.concatenate([cA, cA, -sA, sA, cB, cB, -sB, sB], axis=1).astype(np.float32)
    c["ropetab"] = np.ascontiguousarray(rt.reshape(NT, 128, 384))
    c["ident"] = np.eye(128, dtype=np.float32).astype(bf)
    p = np.arange(128)
    c["tri"] = (p[:, None] <= p[None, :]).astype(np.float32).astype(bf)
    c["atri"] = (p[None, :] < p[:, None]).astype(np.float32).astype(bf)
    vis = np.concatenate([(p[:, None] <= p[None, :]), (p[None, :] < p[:, None])], axis=1).astype(np.float32)
    c["ntrat"] = ((vis - 1.0) * 30000.0).astype(bf)
    cc = np.arange(128)
    cm = ((cc[:, None] * 16 + 31) <= np.arange(S)[None, :]).astype(np.float32)
    cm[127, :] = 0.0
    c["cmask"] = ((cm - 1.0) * 30000.0).astype(bf)
    cs = np.arange(127) * 16
    js = np.arange(32) * 64
    ov = ((cs[:, None] < js[None, :] + 64) & (cs[:, None] + 32 > js[None, :])).astype(np.float32)
    ovp = np.zeros((128, 32), np.float32)
    ovp[:127] = ov
    c["overlap"] = ovp.astype(bf)
    es = (np.arange(S)[None, :] // 64 == np.arange(32)[:, None]).astype(np.float32)
    esp = np.zeros((128, S), np.float32)
    esp[:32] = es
    c["esel"] = esp.astype(bf)
    t = np.arange(S)[:, None]
    j = np.arange(32)[None, :]
    cur = t // 64
    forced = (j == 0) | (j == cur) | (j == cur - 1)
    valid = (j * 64) <= t
    vmask = (valid & ~forced).astype(np.float32)
    cconst = np.where(forced, 1e4, np.where(valid, 0.0, -1.0)).astype(np.float32)
    sc = np.stack([vmask, cconst], axis=1)
    c["selc"] = np.ascontiguousarray(sc.reshape(NT, 128, 64))
    return c


def _split_w_in(w_in):
    sizes = (1024, 256, 256, 256, 256, 256, 256, 1024, 24, 1024, 128, 128, 1024)
    offs = np.cumsum((0,) + sizes)
    names = ["qa", "kca", "vca", "ksa", "vsa", "kwa", "vwa", "za", "ga", "qb", "kb", "vb", "zb"]
    cols = {n: w_in[:, offs[i]:offs[i + 1]] for i, n in enumerate(names)}
    w_kv = np.concatenate([cols["kca"], cols["ksa"], cols["kwa"], cols["kb"], cols["vb"],
                           cols["vca"], cols["vsa"], cols["vwa"]], axis=1)
    w_q1 = np.concatenate([cols["qa"], cols["ga"]], axis=1)
    w_q2 = np.concatenate([cols["za"], cols["qb"]], axis=1)
    w_z2 = cols["zb"]
    return (np.ascontiguousarray(w_kv), np.ascontiguousarray(w_q1),
            np.ascontiguousarray(w_q2), np.ascontiguousarray(w_z2))


def build(debug=None, stop_after=None):
    debug = debug or {}
    nc = bass.Bass("TRN2", target_bir_lowering=False)
    ctx = ExitStack()

    def din(name, shape, dt=F32):
        return nc.dram_tensor(name, list(shape), dt, kind="ExternalInput").ap()

    x_d = din("x", [S, D])
    wkv_d = din("w_kv", [D, 1792])
    wq1_d = din("w_q1", [D, 1048])
    wq2_d = din("w_q2", [D, 2048])
    wz2_d = din("w_z2", [D, 1024])
    wout_d = din("w_out", [D, D])
    ckw1_d = din("cmp_k_w1", [4096, 256])
    ckw2_d = din("cmp_k_w2", [256, 128])
    cvw1_d = din("cmp_v_w1", [4096, 256])
    cvw2_d = din("cmp_v_w2", [256, 128])
    kposT_d = din("kposT", [128, 32])
    vposT_d = din("vposT", [128, 32])
    sinks_d = din("sinks_bc", [128, 16])
    gcol_d = din("g_col", [128, 16])
    fg_d = din("fg_bc", [128, D])
    ropetab_d = din("ropetab", [NT, 128, 384])
    ident_d = din("ident", [128, 128], BF16)
    tri_d = din("tri", [128, 128], BF16)
    atri_d = din("atri", [128, 128], BF16)
    ntrat_d = din("ntrat", [128, 256], BF16)
    cmask_d = din("cmask", [128, S], BF16)
    overlap_d = din("overlap", [128, 32], BF16)
    esel_d = din("esel", [128, S], BF16)
    selc_d = din("selc", [NT, 128, 64])
    out_d = nc.dram_tensor("out", [S, D], F32, kind="ExternalOutput").ap()
    dbg_d = {}

    def dout(name, shape, dt=F32):
        dbg_d[name] = nc.dram_tensor(name, list(shape), dt, kind="ExternalOutput").ap()
        return dbg_d[name]

    PE = Eng(nc, ctx, nc.tensor, "pe")
    ACT = Eng(nc, ctx, nc.scalar, "act")
    DVE = Eng(nc, ctx, nc.vector, "dve")
    POOL = Eng(nc, ctx, nc.gpsimd, "pool")
    SP = Eng(nc, ctx, nc.sync, "sp")
    ENGS = [PE, ACT, DVE, POOL, SP]
    all_tiles = []
    all_dsems = []

    def TT(name, excl=False):
        t = T(name, excl)
        all_tiles.append(t)
        return t

    def dsem(name):
        d = DmaSem(ctx.enter_context(nc.semaphore("d_" + name)))
        all_dsems.append(d)
        return d

    def barrier():
        for E in ENGS:
            for F in ENGS:
                if F is not E and F.n > 0:
                    E.wait(Ev(F, F.n))
            for d in all_dsems:
                if d.n > 0:
                    E.wait(Ev(d, d.n))
        for t in all_tiles:
            t.w = None
            t.r = []

    def sb(name, shape, dt):
        return ctx.enter_context(nc.sbuf_tensor(name, list(shape), dt))

    def ps(name, shape, dt):
        return ctx.enter_context(nc.psum_tensor(name, list(shape), dt))

    def flat(ap3):
        return ap3.rearrange("p a b -> p (a b)")

    QY = sb("QY", [128, 16, S], BF16)
    R2 = sb("R2", [128, 16, S], BF16)
    RLO = flat(R2[:, 0:8, :]).rearrange("p (k c) -> p k c", k=16)
    RHI = flat(R2[:, 8:16, :]).rearrange("p (k c) -> p k c", k=16)
    KT_A = R2[:, 0:6, :]
    VcaT = R2[:, 6:8, :]
    VS = flat(R2[:, 8:11, :])[:, 0:4128].rearrange("p (t k d) -> p t k d", t=NT, k=2)
    VW = flat(R2[:, 11:14, :])[:, 0:4128].rearrange("p (t k d) -> p t k d", t=NT, k=2)
    KT_B = sb("KT_B", [128, 2, S], BF16)
    VB = sb("VB", [128, NT, 2, 65], BF16)
    XS = [sb(f"xs{i}", [128, D], F32) for i in range(2)]
    HB = sb("hb", [128, D], BF16)
    HT = [sb(f"hT{i}", [128, 16, 128], BF16) for i in range(2)]
    RT = [sb(f"rt{i}", [128, 384], F32) for i in range(2)]
    SCR = sb("scr", [128, 7168], BF16)
    TA = SCR[:, 0:1024].bitcast(F32)
    TB = SCR[:, 1024:2048].bitcast(F32)
    KR = [SCR[:, 2048 + i * 512:2560 + i * 512] for i in range(4)]
    PX = [SCR[:, i * 512:(i + 1) * 512] for i in range(4)] + [sb("px4", [128, 512], BF16)]
    ACC = [SCR[:, 2048 + g * 1024:3072 + g * 1024].bitcast(F32).rearrange("p (t d) -> p t d", t=4) for g in range(4)]
    OBF = SCR[:, 2048:2560].rearrange("p (t d) -> p t d", t=4)
    TMPA = SCR[:, 6144:7168].bitcast(F32).rearrange("p (t d) -> p t d", t=4)
    IDENT = sb("ident_s", [128, 128], BF16)
    TRI = sb("tri_s", [128, 128], BF16)
    ATRI = sb("atri_s", [128, 128], BF16)
    TRAT01 = sb("trat01_s", [128, 256], BF16)
    TRAT = TRAT01
    CMASK = sb("cmask_s", [128, S], BF16)
    ESEL = sb("esel_s", [128, S], BF16)
    SELB = sb("selb", [128, 2, 1024], BF16)
    GAW = sb("gaw", [128, 16, 24], BF16)
    GATES = sb("gates", [128, NT, 24], F32)
    KCT = sb("kct", [128, 2, 128], BF16)
    VCX = sb("vcx", [128, 2, 168], BF16)
    W2K = sb("w2k", [128, 2, 128], BF16)
    W2V = sb("w2v", [128, 2, 128], BF16)
    KPOS = sb("kpos", [128, 32], F32)
    VPOS = sb("vpos", [128, 32], F32)
    GCOL = sb("gcol", [128, 16], F32)
    RSTD = sb("rstd", [128, NT], F32)
    SS = sb("ss", [128, NT], F32)
    SS2 = sb("ss2", [128, NT], F32)
    RS2 = sb("rs2", [128, NT], F32)
    EPS_T = sb("eps_t", [128, 1], F32)
    ESINK = sb("esink", [128, 16], F32)
    SELCQ = sb("selcq", [128, 4, 64], F32)
    PSEL = sb("psel", [128, 4, 32], F32)
    SCO = sb("sco", [128, 4, 32], F32)
    SC2 = sb("sc2", [128, 32], F32)
    M8 = sb("m8", [128, 8], F32)
    M8b = sb("m8b", [128, 8], F32)
    M8B4 = sb("m8b4", [128, 4, 8], F32)
    NB = sb("nb", [128, 4, 32], BF16)
    RD = sb("rd", [128, 8], F32)
    RD2 = sb("rd2", [128, 8], F32)
    GW = sb("gw", [128, 4], F32)
    TMPS = sb("tmps", [128, 4, 32], F32)

    _b012 = [ps(f"bk{i}", [128, 512], F32) for i in range(3)]
    PVT = [ps(f"pvt{i}", [128, 1024], F32) for i in range(2)]
    _b7 = ps("bk7", [128, 512], F32)
    BK = [_b012[0][:], _b012[1][:], _b012[2][:], PVT[0][:, 0:512], PVT[0][:, 512:1024],
          PVT[1][:, 0:512], PVT[1][:, 512:1024], _b7[:]]
    BKb = [BK[i].bitcast(BF16) for i in range(8)]
    PM = BK[0:4]

    t_xs = [TT("xs0"), TT("xs1")]
    t_hb = TT("hb")
    t_ht = [TT("ht0"), TT("ht1")]
    t_rt = [TT("rt0"), TT("rt1")]
    t_ta, t_tb = TT("ta"), TT("tb")
    t_kr = [TT(f"kr{i}") for i in range(4)]
    t_bk = [TT(f"bk{i}", True) for i in range(8)]
    t_pm = t_bk[0:4]
    t_slabb = [TT(f"slab{i}") for i in range(5)]
    t_const = TT("const")
    t_rstd = TT("rstd")
    t_kv = TT("kv")
    t_q = [[TT(f"q{c}_{q}") for q in range(NQG)] for c in range(16)]
    t_gates = TT("gates")
    t_px = [TT(f"px{i}") for i in range(5)]
    t_acc = [TT(f"acc{g}") for g in range(4)]
    t_obf = TT("obf")
    t_rd = TT("rd")
    t_tmpa = TT("tmpa")
    t_sel = TT("sel")
    t_selb = TT("selb")
    t_selc = TT("selc")
    t_cmpw = TT("cmpw")
    t_blk = [TT(f"blk{i}") for i in range(4)]
    t_hid = [TT(f"hid{i}") for i in range(8)]
    t_kc = TT("kc")

    d_xs = [dsem("xs0"), dsem("xs1")]
    d_rt = [dsem("rt0"), dsem("rt1")]
    d_const = dsem("const")
    d_slab = dsem("slab")
    d_slabb = [dsem(f"slabb{i}") for i in range(5)]
    d_out = [dsem("out0"), dsem("out1")]
    d_selc = dsem("selc")

    t_const2 = TT("const2")
    d_const2 = dsem("const2")
    for dst, src in ((IDENT, ident_d), (GCOL, gcol_d)):
        dma(SP, d_const, dst[:], src, writes=[t_const])

    def late_consts():
        dma(SP, d_const2, TRAT01[:, 0:128], tri_d, writes=[t_const2])
        dma(SP, d_const2, TRAT01[:, 128:256], atri_d, writes=[t_const2])
        for dst, src in ((TRI, tri_d), (ATRI, atri_d), (CMASK, cmask_d), (ESEL, esel_d),
                         (KPOS, kposT_d), (VPOS, vposT_d), (ESINK, sinks_d)):
            dma(SP, d_const2, dst[:], src, writes=[t_const2])

    op(DVE, lambda e: e.memset(EPS_T[:], RMS_EPS), writes=[t_const])
    op(DVE, lambda e: e.memset(SS[:], 0.0), writes=[t_rstd])
    op(DVE, lambda e: e.memset(SS2[:], 0.0), writes=[t_rstd])
    op(DVE, lambda e: e.memset(SELB[:], 0.0), writes=[t_selb])
    op(DVE, lambda e: e.memset(VS[:, :, :, 128:129], 1.0), writes=[t_kv])
    op(DVE, lambda e: e.memset(VW[:, :, :, 128:129], 1.0), writes=[t_kv])
    op(DVE, lambda e: e.memset(VB[:, :, :, 64:65], 1.0), writes=[t_kv])

    def load_slab(dst, w_d, c0, ncols, blk0=0, dcol0=0, cb=512):
        wv = w_d.rearrange("(kc p) c -> p kc c", p=128)
        for j, cc in enumerate(range(0, ncols, cb)):
            w = min(cb, ncols - cc)
            for kc in range(0, 16, 8):
                dma(POOL, d_slabb[blk0 + j], dst[:, kc:kc + 8, dcol0 + cc:dcol0 + cc + w],
                    wv[:, kc:kc + 8, c0 + cc:c0 + cc + w], writes=[t_slabb[blk0 + j]])

    xt_count = [0]
    pm_count = [0]
    pending = []
    pre_bufs = {}

    def preissue_x(need_rt):
        xv0 = x_d.rearrange("(t p) d -> t p d", p=128)
        for tt in range(2):
            b = xt_count[0] % 2
            xt_count[0] += 1
            dma(SP, d_xs[b], XS[b][:], xv0[tt], writes=[t_xs[b]])
            if need_rt:
                dma(SP, d_rt[b], RT[b][:], ropetab_d[tt], writes=[t_rt[b]])
            pre_bufs[tt] = b

    def flush_pending():
        while pending:
            pending.pop(0)()

    def proj_pass(blocks, post, first=False, need_rt=False):
        xv = x_d.rearrange("(t p) d -> t p d", p=128)
        bufs = {}

        def issue_x(tt):
            b = xt_count[0] % 2
            xt_count[0] += 1
            dma(SP, d_xs[b], XS[b][:], xv[tt], writes=[t_xs[b]])
            bufs[tt] = b

        def issue_rt(tt):
            if need_rt:
                b = bufs[tt]
                dma(SP, d_rt[b], RT[b][:], ropetab_d[tt], writes=[t_rt[b]])

        def prep_act(tt):
            b = bufs[tt]
            if first:
                op(ACT, lambda e: e.activation(out=HB[:], in_=XS[b][:], func=ACTF.Square,
                                               accum_out=SS[:, tt:tt + 1]),
                   reads=[t_xs[b]], writes=[t_hb, t_rstd])
                op(ACT, lambda e: e.activation(out=SS[:, tt:tt + 1], in_=SS[:, tt:tt + 1], func=ACTF.Sqrt,
                                               scale=1.0 / D, bias=EPS_T[:, 0:1]),
                   reads=[t_const], writes=[t_rstd])
                op(DVE, lambda e: e.reciprocal(out=RSTD[:, tt:tt + 1], in_=SS[:, tt:tt + 1]),
                   writes=[t_rstd])
            op(ACT, lambda e: e.mul(out=HB[:], in_=XS[b][:], mul=RSTD[:, tt:tt + 1]),
               reads=[t_xs[b], t_rstd], writes=[t_hb])
            if tt + 2 < NT:
                issue_x(tt + 2)

        def prep_pe(tt):
            hb_i = tt % 2
            for half in range(2):
                bk = 4 + half
                grp(PE, [(lambda e, kc=kc: e.transpose(out=BKb[bk][:, (kc % 8) * 128:(kc % 8 + 1) * 128],
                                                        in_=HB[:, kc * 128:(kc + 1) * 128],
                                                        identity=IDENT[:]))
                         for kc in range(half * 8, half * 8 + 8)],
                    reads=[t_hb, t_const], writes=[t_bk[bk]])
                if half == 0 and not first:
                    grp(ACT, [(lambda e, kc=kc: e.mul(out=HT[hb_i][:, kc, :], in_=BKb[bk][:, kc * 128:(kc + 1) * 128],
                                                      mul=GCOL[:, kc:kc + 1])) for kc in range(8)],
                        reads=[t_bk[bk], t_const], writes=[t_ht[hb_i]])
                else:
                    op(DVE, lambda e: e.tensor_tensor(out=HT[hb_i][:, half * 8:half * 8 + 8, :],
                                                      in0=BKb[bk].rearrange("p (k t) -> p k t", k=8),
                                                      in1=bc_last(GCOL[:, half * 8:half * 8 + 8], 128), op=ALU.mult),
                       reads=[t_bk[bk], t_const], writes=[t_ht[hb_i]])

        if pre_bufs:
            bufs.update(pre_bufs)
            pre_bufs.clear()
        else:
            issue_x(0)
            issue_rt(0)
            issue_x(1)
            issue_rt(1)
        prep_act(0)
        prep_pe(0)
        nb = len(blocks)
        for tt in range(NT):
            b = bufs[tt]
            hb_i = tt % 2
            if tt + 1 < NT:
                prep_act(tt + 1)
            for bi, (slab, c0, w, sti) in enumerate(blocks):
                pi = pm_count[0] % 4
                pm_count[0] += 1
                pm = PM[pi]
                grp(PE, [(lambda e, kc=kc: e.matmul(pm[:, 0:w], lhsT=HT[hb_i][:, kc, :],
                                                    rhs=slab[:, kc, c0:c0 + w],
                                                    start=(kc == 0), stop=(kc == 15)))
                         for kc in range(16)],
                    reads=[t_ht[hb_i], t_slabb[sti]], writes=[t_pm[pi]])
                if w >= 256:
                    flush_pending()
                post(tt, bi, pm, t_pm[pi], w, b)
                if bi == max(0, nb - 3) and tt + 1 < NT:
                    prep_pe(tt + 1)
            if tt + 2 < NT:
                issue_rt(tt + 2)
            if first and tt == 4:
                late_consts()
        flush_pending()

    def rope(pm, t_pmi, c0, nh, d, rt_b, outs):
        hd = d // 2
        base = 0 if d == 128 else 256
        t_rtb = t_rt[rt_b]
        cos2 = RT[rt_b][:, base:base + d]
        nsin = RT[rt_b][:, base + d:base + d + hd]
        psin = RT[rt_b][:, base + d + hd:base + 2 * d]
        w = nh * d
        psv = pm[:, c0:c0 + w].rearrange("p (h d) -> p h d", h=nh)
        tav = TA[:, 0:w].rearrange("p (h d) -> p h d", h=nh)
        tbv = TB[:, 0:w].rearrange("p (h d) -> p h d", h=nh)
        op(DVE, lambda e: e.tensor_tensor(out=tav, in0=psv, in1=bc_mid(cos2, nh), op=ALU.mult),
           reads=[t_pmi, t_rtb], writes=[t_ta])
        op(DVE, lambda e: e.tensor_tensor(out=tbv[:, :, 0:hd], in0=psv[:, :, hd:d], in1=bc_mid(nsin, nh), op=ALU.mult),
           reads=[t_pmi, t_rtb], writes=[t_tb])
        op(DVE, lambda e: e.tensor_tensor(out=tbv[:, :, hd:d], in0=psv[:, :, 0:hd], in1=bc_mid(psin, nh), op=ALU.mult),
           reads=[t_pmi, t_rtb], writes=[t_tb])
        for (o, t_o) in outs:
            op(DVE, lambda e, o=o: e.tensor_tensor(out=o, in0=tav, in1=tbv, op=ALU.add),
               reads=[t_ta, t_tb], writes=[t_o])

    kr_cnt = [0, 0]

    def transposes(src_tile, t_src, nchunks, dsts, eng=None):
        def run():
            bk = 6 + kr_cnt[1] % 2
            kr_cnt[1] += 1
            grp(PE, [(lambda e, c=c: e.transpose(out=BKb[bk][:, c * 128:(c + 1) * 128],
                                                  in_=src_tile[:, c * 128:(c + 1) * 128], identity=IDENT[:]))
                     for c in range(nchunks)],
                reads=[t_src, t_const], writes=[t_bk[bk]])
            src = BKb[bk][:, 0:nchunks * 128].rearrange("p (k t) -> p k t", k=nchunks)
            for (dst_ap, lo, hi, tl) in dsts:
                op(ACT, lambda e, dst_ap=dst_ap, lo=lo, hi=hi: e.copy(out=dst_ap, in_=src[:, lo:hi, :]),
                   reads=[t_bk[bk]], writes=tl)
        kr_cnt[0] += 1
        pending.append(run)

    load_slab(QY, wkv_d, 0, 1792)
    t_w1k = TT("w1k")
    d_w1k = dsem("w1k")
    dma(POOL, d_w1k, W2K[:], ckw2_d.rearrange("(c j) d -> j c d", j=128), writes=[t_w1k])
    dma(POOL, d_w1k, W2V[:], cvw2_d.rearrange("(c j) d -> j c d", j=128), writes=[t_w1k])
    wvk = ckw1_d.rearrange("(l d) j -> d l j", d=128)
    W1KA = flat(R2[:, 14:16, :]).rearrange("p (l j) -> p l j", l=16)
    W1KB = SCR[:, 4096:7168].rearrange("p (l j) -> p l j", l=12)
    W1KC = flat(R2[:, 8:11, :])[:, 4128:5152].rearrange("p (l j) -> p l j", l=4)
    dma(POOL, d_w1k, W1KA[:, 0:8, :], wvk[:, 0:8, :], writes=[t_w1k])
    dma(POOL, d_w1k, W1KA[:, 8:16, :], wvk[:, 8:16, :], writes=[t_w1k])
    dma(POOL, d_w1k, W1KB[:, 0:8, :], wvk[:, 16:24, :], writes=[t_w1k])
    dma(POOL, d_w1k, W1KB[:, 8:12, :], wvk[:, 24:28, :], writes=[t_w1k])
    dma(POOL, d_w1k, W1KC, wvk[:, 28:32, :], writes=[t_w1k])
    W1K_L = [W1KA[:, l, :] for l in range(16)] + [W1KB[:, l, :] for l in range(12)] + [W1KC[:, l, :] for l in range(4)]

    def post1(tt, blk, pm, t_pmi, w, b):
        ts_ = slice(tt * 128, (tt + 1) * 128)
        ki = kr_cnt[0] % 4
        kr = KR[ki]
        if blk == 0:
            rope(pm, t_pmi, 0, 4, 128, b, [(kr[:, 0:512].rearrange("p (h d) -> p h d", h=4), t_kr[ki])])
            transposes(kr, t_kr[ki], 4, [(KT_A[:, 0:4, ts_], 0, 4, [t_kv])])
        elif blk == 1:
            rope(pm, t_pmi, 0, 2, 128, b, [(kr[:, 0:256].rearrange("p (h d) -> p h d", h=2), t_kr[ki])])
            kdup = kr[:, 256:512].rearrange("p (k c d) -> p k c d", k=2, c=2)
            rope(pm, t_pmi, 256, 2, 64, b, [(kdup[:, :, 0, :], t_kr[ki]), (kdup[:, :, 1, :], t_kr[ki])])
            op(ACT, lambda e: e.copy(out=VB[:, tt, :, 0:64], in_=pm[:, 384:512].rearrange("p (k d) -> p k d", k=2)),
               reads=[t_pmi], writes=[t_kv])
            transposes(kr, t_kr[ki], 4, [(KT_A[:, 4:6, ts_], 0, 2, [t_kv]), (KT_B[:, 0:2, ts_], 2, 4, [t_kv])])
        elif blk == 2:
            op(ACT, lambda e: e.copy(out=kr[:, 0:256], in_=pm[:, 0:256]), reads=[t_pmi], writes=[t_kr[ki]])
            op(ACT, lambda e: e.copy(out=VS[:, tt, :, 0:128], in_=pm[:, 256:512].rearrange("p (k d) -> p k d", k=2)),
               reads=[t_pmi], writes=[t_kv])
            transposes(kr, t_kr[ki], 2, [(VcaT[:, 0:2, ts_], 0, 2, [t_kv])])
        else:
            op(ACT, lambda e: e.copy(out=VW[:, tt, :, 0:128], in_=pm[:, 0:256].rearrange("p (k d) -> p k d", k=2)),
               reads=[t_pmi], writes=[t_kv])

    proj_pass([(QY, 0, 512, 0), (QY, 512, 512, 1), (QY, 1024, 512, 2), (QY, 1536, 256, 3)], post1, first=True, need_rt=True)
    barrier()
    W1V = [XS[0][:].bitcast(BF16).rearrange("p (l j) -> p l j", l=16),
           XS[1][:].bitcast(BF16).rearrange("p (l j) -> p l j", l=16)]
    wvv = cvw1_d.rearrange("(l d) j -> d l j", d=128)
    for hf in range(2):
        for l0 in range(0, 16, 8):
            dma(POOL, d_slabb[4], W1V[hf][:, l0:l0 + 8, :], wvv[:, hf * 16 + l0:hf * 16 + l0 + 8, :],
                writes=[t_slabb[4]])
    W1L = [W1K_L, [W1V[l // 16][:, l % 16, :] for l in range(32)]]
    t_w1 = [t_w1k, t_slabb[4]]
    SL2 = flat(QY[:, 8:16, :]).rearrange("p (k c) -> p k c", k=16)
    load_slab(SL2, wq1_d, 0, 1024)
    load_slab(GAW, wq1_d, 1024, 24, blk0=2)

    op(DVE, lambda e: e.memset(VCX[:], 0.0), writes=[t_kc])
    op(DVE, lambda e: e.memset(KCT[:], 0.0), writes=[t_kc])
    op(DVE, lambda e: e.memset(VCX[:, :, 128:129], 1.0), writes=[t_kc])
    for k in range(2):
        dma(SP, d_const, VCX[:, k, 129:161], overlap_d, writes=[t_kc])
    BLK = [flat(QY[:, 2 * x:2 * x + 2, :])[:, 0:4064].rearrange("p (l c) -> p l c", l=32) for x in range(4)]
    HID = [SCR[:, i * 128:i * 128 + 127] for i in range(8)]
    for x, (src, h, pos) in enumerate(((KT_A, 0, KPOS), (KT_A, 1, KPOS), (VcaT, 0, VPOS), (VcaT, 1, VPOS))):
        s2 = src[:, h, :]
        a = s2.ap
        srcv = bass.AP(s2.tensor, s2.offset, [list(a[0]), [1, 32], [16, 127]])
        if x % 2 == 0:
            op(DVE, lambda e, x=x, srcv=srcv, pos=pos: e.tensor_tensor(out=BLK[x], in0=srcv, in1=bc_last(pos[:, 0:32], 127), op=ALU.add),
               reads=[t_const], writes=[t_blk[x]])
        else:
            grp(ACT, [(lambda e, x=x, l=l, s2=s2, a=a, pos=pos: e.activation(
                out=BLK[x][:, l, :], in_=bass.AP(s2.tensor, s2.offset + l, [list(a[0]), [16, 127]]),
                func=ACTF.Identity, bias=pos[:, l:l + 1])) for l in range(32)],
                reads=[t_const], writes=[t_blk[x]])
    for x in range(4):
        kv = x // 2
        for jc in range(2):
            pi = pm_count[0] % 4
            pm_count[0] += 1
            grp(PE, [(lambda e, l=l: e.matmul(PM[pi][:, 0:127], lhsT=W1L[kv][l][:, jc * 128:(jc + 1) * 128],
                                              rhs=BLK[x][:, l, :], start=(l == 0), stop=(l == 31)))
                     for l in range(32)],
                reads=[t_w1[kv], t_blk[x]], writes=[t_pm[pi]])
            op(ACT, lambda e: e.activation(out=HID[x * 2 + jc], in_=PM[pi][:, 0:127], func=ACTF.Silu),
               reads=[t_pm[pi]], writes=[t_hid[x * 2 + jc]])
    for x in range(4):
        h = x % 2
        pi = pm_count[0] % 4
        pm_count[0] += 1
        if x < 2:
            grp(PE, [(lambda e, jc=jc: e.matmul(PM[pi][:, 0:127], lhsT=W2K[:, jc, :], rhs=HID[x * 2 + jc],
                                                start=(jc == 0), stop=(jc == 1))) for jc in range(2)],
                reads=[t_w1k, t_hid[x * 2], t_hid[x * 2 + 1]], writes=[t_pm[pi]])
            op(ACT, lambda e: e.copy(out=KCT[:, h, 0:127], in_=PM[pi][:, 0:127]), reads=[t_pm[pi]], writes=[t_kc])
        else:
            grp(PE, [(lambda e, jc=jc: e.matmul(PM[pi][0:127, 0:128], lhsT=HID[x * 2 + jc], rhs=W2V[:, jc, :],
                                                start=(jc == 0), stop=(jc == 1))) for jc in range(2)],
                reads=[t_w1k, t_hid[x * 2], t_hid[x * 2 + 1]], writes=[t_pm[pi]])
            op(ACT, lambda e: e.copy(out=VCX[0:127, h, 0:128], in_=PM[pi][0:127, 0:128]), reads=[t_pm[pi]], writes=[t_kc])
    barrier()

    if "kv" in debug:
        dd = dout("dbg_KCT", [128, 2, 128], BF16)
        dma(SP, d_out[0], dd, KCT[:], reads=[t_kc])
        dd = dout("dbg_VCX", [128, 2, 168], BF16)
        dma(SP, d_out[0], dd, VCX[:], reads=[t_kc])
        barrier()


    def post2(tt, blk, pm, t_pmi, w, b):
        ts_ = slice(tt * 128, (tt + 1) * 128)
        Q = tt // 4
        if blk < 2:
            ki = kr_cnt[0] % 4
            kr = KR[ki]
            rope(pm, t_pmi, 0, 4, 128, b, [(kr[:, 0:512].rearrange("p (h d) -> p h d", h=4), t_kr[ki])])
            transposes(kr, t_kr[ki], 4, [(QY[:, blk * 4:blk * 4 + 4, ts_], 0, 4, [t_q[blk * 4 + j][Q] for j in range(4)])])
        else:
            op(ACT, lambda e: e.activation(out=GATES[:, tt, :], in_=pm[:, 0:24], func=ACTF.Sigmoid),
               reads=[t_pmi], writes=[t_gates])

    proj_pass([(SL2, 0, 512, 0), (SL2, 512, 512, 1), (GAW, 0, 24, 2)], post2, need_rt=True)
    barrier()

    SCB = [0, 1, 2, 7]
    PVB = [(3, 4), (5, 6)]
    TRB = 7
    LA = 4
    st_cnt = [0]
    un_cnt = [0]

    class Unit:
        pass

    def run_units(units):
        flat = []
        for u in units:
            u.pair = un_cnt[0] % 2
            un_cnt[0] += 1
            u.first = [True, True]
            for si in range(len(u.steps)):
                flat.append((u, si))
        n = len(flat)
        slots = {}
        for idx in range(n + LA):
            if idx < n:
                u, si = flat[idx]
                g = st_cnt[0]
                st_cnt[0] += 1
                slots[idx] = g
                if si == 0 and getattr(u, "pre", None) is not None:
                    u.pre()
                u.emit_s(u, u.steps[si], g, g % 5)
            j = idx - LA
            if j >= 0:
                u, si = flat[j]
                g = slots.pop(j)
                u.emit_pv(u, u.steps[si], g % 5)
                if si == len(u.steps) - 1:
                    u.finish(u)

    inv_sqrt_a = 1.0 / math.sqrt(128.0)

    def a_emit_s(u, st, gstep, xi):
        segs, masks = st
        sb_ = SCB[gstep % 4]
        pmS = BK[sb_]
        kr_ = u.krows
        LO = min(sg[1] for sg in segs)
        HI = max(sg[2] for sg in segs)
        fns = []
        for (kt, lo, hi) in segs:
            fns.append(lambda e, kt=kt, lo=lo, hi=hi: e.matmul(pmS[0:kr_, lo:hi], lhsT=u.lhsT(kt), rhs=u.q(lo, hi),
                                                             start=True, stop=(u.bias is None)))
            if u.bias is not None:
                bl, br_ = u.bias(kt, lo, hi)
                fns.append(lambda e, lo=lo, hi=hi, bl=bl, br_=br_: e.matmul(pmS[0:kr_, lo:hi], lhsT=bl, rhs=br_,
                                                                           start=False, stop=True))
        grp(PE, fns, reads=[u.tq, t_selb, t_const], writes=[t_bk[sb_]])
        px = PX[xi]
        op(ACT, lambda e: e.activation(out=px[0:kr_, LO:HI], in_=pmS[0:kr_, LO:HI], func=ACTF.Exp, scale=u.scale),
           reads=[t_bk[sb_]], writes=[t_px[xi]])
        for (i, m) in masks:
            mk = TRI if m == "tri" else ATRI
            ME = POOL
            op(ME, lambda e, i=i, mk=mk: e.tensor_tensor(out=px[:, i * 128:(i + 1) * 128],
                                                         in0=px[:, i * 128:(i + 1) * 128],
                                                         in1=mk[:], op=ALU.mult),
               reads=[t_const], writes=[t_px[xi]])

    def a_emit_pv(u, st, xi):
        segs, masks = st
        px = PX[xi]
        banks = [BK[PVB[u.pair][0]], BK[PVB[u.pair][1]]]
        tb = [t_bk[PVB[u.pair][0]], t_bk[PVB[u.pair][1]]]
        fns = []
        used = set()
        for (kt, lo, hi) in segs:
            vv = u.v(kt)
            vw = vv.shape[-1]
            for i in range(lo // 128, hi // 128):
                bk = i // 2
                stf = u.first[bk]
                u.first[bk] = False
                used.add(bk)
                fns.append(lambda e, i=i, bk=bk, stf=stf, vv=vv, vw=vw: e.matmul(
                    banks[bk][:, (i % 2) * 256:(i % 2) * 256 + vw], lhsT=px[0:u.krows, i * 128:(i + 1) * 128],
                    rhs=vv, start=stf, stop=False, skip_group_check=True))
        grp(PE, fns, reads=[t_px[xi], t_kv, t_kc], writes=[tb[bk] for bk in sorted(used)])

    def a_finish(u):
        pr = u.pair
        tb = [t_bk[PVB[pr][0]], t_bk[PVB[pr][1]]]
        h, Q, br, g = u.h, u.Q, u.br, u.g
        pv4 = PVT[pr][:].rearrange("p (t c) -> p t c", t=4)
        if br == 0:
            op(DVE, lambda e: e.tensor_scalar(out=RD[:, 0:4], in0=pv4[:, :, 128], scalar1=TINY, scalar2=None, op0=ALU.max),
               reads=tb, writes=[t_rd])
            op(DVE, lambda e: e.reciprocal(out=RD2[:, 0:4], in_=RD[:, 0:4]), writes=[t_rd])
        else:
            op(DVE, lambda e: e.reciprocal(out=RD2[:, 0:4], in_=pv4[:, :, 128]), reads=tb, writes=[t_rd])
        op(DVE, lambda e: e.tensor_tensor(out=GW[:], in0=RD2[:, 0:4], in1=GATES[:, 4 * Q:4 * Q + 4, h * 3 + br], op=ALU.mult),
           reads=[t_gates], writes=[t_rd])
        if u.first_branch:
            op(DVE, lambda e: e.tensor_tensor(out=ACC[g], in0=pv4[:, :, 0:128], in1=bc_last(GW[:], 128), op=ALU.mult),
               reads=tb + [t_rd], writes=[t_acc[g]])
        else:
            op(DVE, lambda e: e.tensor_tensor(out=TMPA, in0=pv4[:, :, 0:128], in1=bc_last(GW[:], 128), op=ALU.mult),
               reads=tb + [t_rd], writes=[t_tmpa])
            if br == 1:
                op(DVE, lambda e: e.tensor_tensor(out=QY[:, h, Q * 512:(Q + 1) * 512].rearrange("p (t d) -> p t d", t=4),
                                                  in0=TMPA, in1=ACC[g], op=ALU.add),
                   reads=[t_tmpa, t_acc[g]], writes=[u.tq])
            else:
                op(DVE, lambda e: e.tensor_tensor(out=ACC[g], in0=TMPA, in1=ACC[g], op=ALU.add),
                   reads=[t_tmpa], writes=[t_acc[g]])
        if u.want_psel:
            if g == 0:
                op(DVE, lambda e: e.tensor_tensor(out=PSEL[:], in0=pv4[:, :, 129:161], in1=bc_last(RD2[:, 0:4], 32), op=ALU.mult),
                   reads=tb + [t_rd], writes=[t_sel])
            else:
                op(DVE, lambda e: e.tensor_tensor(out=TMPS[:], in0=pv4[:, :, 129:161], in1=bc_last(RD2[:, 0:4], 32), op=ALU.mult),
                   reads=tb + [t_rd], writes=[t_tmpa])
                op(DVE, lambda e: e.tensor_tensor(out=PSEL[:], in0=PSEL[:], in1=TMPS[:], op=ALU.add),
                   reads=[t_tmpa], writes=[t_sel])
        if u.after is not None:
            u.after()

    def selection(k, Q):
        op(DVE, lambda e: e.tensor_tensor(out=SCO[:], in0=PSEL[:], in1=SELCQ[:, :, 0:32], op=ALU.mult),
           reads=[t_selc], writes=[t_sel])
        op(DVE, lambda e: e.tensor_tensor(out=SCO[:], in0=SCO[:], in1=SELCQ[:, :, 32:64], op=ALU.add),
           reads=[t_selc], writes=[t_sel])
        for i in range(4):
            op(DVE, lambda e: e.max(out=M8[:], in_=SCO[:, i, :]), writes=[t_sel])
            op(DVE, lambda e: e.match_replace(out=SC2[:], in_to_replace=M8[:], in_values=SCO[:, i, :],
                                              imm_value=-2.0), writes=[t_sel])
            op(DVE, lambda e: e.max(out=M8B4[:, i, :], in_=SC2[:]), writes=[t_sel])
        op(DVE, lambda e: e.tensor_tensor(out=TMPS[:], in0=SCO[:], in1=bc_last(M8B4[:, :, 7], 32), op=ALU.is_lt),
           reads=[t_tmpa], writes=[t_sel, t_tmpa])
        op(DVE, lambda e: e.tensor_scalar(out=NB[:], in0=TMPS[:], scalar1=NEGBIG, scalar2=None, op0=ALU.mult),
           reads=[t_tmpa], writes=[t_sel])
        grp(PE, [(lambda e, i=i: e.transpose(out=BKb[TRB][0:32, i * 128:(i + 1) * 128], in_=NB[:, i, :],
                                              identity=IDENT[:])) for i in range(4)],
            reads=[t_sel, t_const], writes=[t_bk[TRB]])
        op(ACT, lambda e: e.copy(out=SELB[0:32, k, (Q - 2) * 512:(Q - 1) * 512], in_=BKb[TRB][0:32, 0:512]),
           reads=[t_bk[TRB]], writes=[t_selb])

    def mk_a_unit(k, Q, g, br):
        u = Unit()
        h = 4 * k + g
        u.h, u.Q, u.br, u.g, u.k = h, Q, br, g, k
        u.tq = t_q[h][Q]
        u.q = lambda lo, hi: QY[:, h, Q * 512 + lo:Q * 512 + hi]
        u.scale = inv_sqrt_a
        u.bias = None
        u.krows = 128
        u.after = None
        u.want_psel = False
        u.first_branch = False
        u.emit_s, u.emit_pv, u.finish = a_emit_s, a_emit_pv, a_finish
        if br == 0:
            u.steps = [([(0, 0, 512)], [])]
            u.bias = lambda kt, lo, hi: (IDENT[:, 0:127], CMASK[:, Q * 512 + lo:Q * 512 + hi])
            u.lhsT = lambda kt: KCT[:, k, 0:127]
            u.v = lambda kt: VCX[0:127, k, 0:161]
            u.krows = 127
            u.first_branch = True
            u.want_psel = Q >= 2
            if g == 3 and Q >= 2:
                u.after = lambda: selection(k, Q)
        elif br == 2:
            steps = []
            if Q == 0:
                for r in range(0, 4):
                    steps.append(([(r, r * 128, 512)], [(r, "tri")]))
            else:
                kt0 = 4 * Q
                for j in range(3):
                    steps.append(([(kt0 - 4 + j, 0, (j + 1) * 128), (kt0 + 1 + j, (j + 1) * 128, 512)],
                                  [(j, "atri"), (j + 1, "tri")]))
                steps.append(([(kt0 - 1, 0, 512)], [(3, "atri")]))
                steps.append(([(kt0, 0, 512)], [(0, "tri")]))
            u.steps = steps
            u.lhsT = lambda kt: KT_A[:, 4 + k, kt * 128:(kt + 1) * 128]
            u.v = lambda kt: VW[:, kt, k, 0:129]
        else:
            steps = []
            for kt in range(0, 4 * Q + 4):
                r = kt - 4 * Q
                if r < 0:
                    steps.append(([(kt, 0, 512)], []))
                else:
                    steps.append(([(kt, r * 128, 512)], [(r, "tri")]))
            u.steps = steps
            u.lhsT = lambda kt: KT_A[:, 2 + k, kt * 128:(kt + 1) * 128]
            u.v = lambda kt: VS[:, kt, k, 0:129]
            if Q >= 2:
                u.bias = lambda kt, lo, hi: (ESEL[:, kt * 128:(kt + 1) * 128],
                                             SELB[:, k, (Q - 2) * 512 + lo:(Q - 2) * 512 + hi])
        return u

    groups = [(k, Q) for k in range(2) for Q in range(NQG)]

    def cmp_unit(n, g):
        k, Q = groups[n]
        u = mk_a_unit(k, Q, g, 0)
        if g == 0 and Q >= 2:
            u.pre = lambda: dma(SP, d_selc, SELCQ[:], selc_d[4 * Q:4 * Q + 4].rearrange("t p c -> p t c"),
                                writes=[t_selc])
        return u

    wv_za = wq2_d.rearrange("(kc p) c -> p kc c", p=128)

    def za_piece(j, kc):
        return lambda: dma(POOL, d_slabb[j], SL2[:, kc:kc + 8, j * 512:(j + 1) * 512],
                           wv_za[:, kc:kc + 8, j * 512:(j + 1) * 512], writes=[t_slabb[j]])
    za_pieces = [za_piece(j, kc) for j in range(2) for kc in (0, 8)]
    units = [cmp_unit(0, g) for g in range(4)]
    for n, (k, Q) in enumerate(groups):
        if Q >= 2:
            for g in range(4):
                units.append(mk_a_unit(k, Q, g, 2))
        for g in range(4):
            if Q < 2:
                units.append(mk_a_unit(k, Q, g, 2))
            usel = mk_a_unit(k, Q, g, 1)
            if k == 0 and Q == 2:
                usel.pre = za_pieces[g]
            units.append(usel)
            if n + 1 < len(groups):
                units.append(cmp_unit(n + 1, g))
    run_units(units)
    preissue_x(True)
    barrier()

    if "oa" in debug:
        dd = dout("dbg_OA", [128, 8, S], BF16)
        dma(SP, d_out[0], dd, QY[:, 0:8, :])
        dd = dout("dbg_GATES", [128, NT, 24], F32)
        dma(SP, d_out[0], dd, GATES[:])
        barrier()

    def post_z(cbase):
        def f(tt, blk, pm, t_pmi, w, b):
            ts_ = slice(tt * 128, (tt + 1) * 128)
            Q = tt // 4
            cb = cbase + blk * 4
            tl = [t_q[cb + j][Q] for j in range(4)]
            op(ACT, lambda e: e.activation(out=TA[:, 0:512], in_=pm[:, 0:512], func=ACTF.Silu),
               reads=[t_pmi], writes=[t_ta])
            ki = kr_cnt[0] % 4
            kr = KR[ki]
            op(DVE, lambda e: e.tensor_tensor(out=kr[:, 0:512].rearrange("p (c d) -> p c d", c=4),
                                              in0=TA[:, 0:512].rearrange("p (c d) -> p c d", c=4),
                                              in1=QY[:, cb:cb + 4, ts_], op=ALU.mult),
               reads=[t_ta] + tl, writes=[t_kr[ki]])
            transposes(kr, t_kr[ki], 4, [(QY[:, cb:cb + 4, ts_], 0, 4, tl)])
        return f

    load_slab(RLO, wq2_d, 1024, 1024, blk0=2)
    pz3 = post_z(0)

    def post3(tt, blk, pm, t_pmi, w, b):
        if blk < 2:
            pz3(tt, blk, pm, t_pmi, w, b)
        else:
            ts_ = slice(tt * 128, (tt + 1) * 128)
            Q = tt // 4
            ki = kr_cnt[0] % 4
            kr = KR[ki]
            cb = 8 + (blk - 2) * 4
            rope(pm, t_pmi, 0, 8, 64, b, [(kr[:, 0:512].rearrange("p (h d) -> p h d", h=8), t_kr[ki])])
            transposes(kr, t_kr[ki], 4, [(R2[:, cb:cb + 4, ts_], 0, 4, [t_q[cb + j][Q] for j in range(4)])])

    proj_pass([(SL2, 0, 512, 0), (SL2, 512, 512, 1), (RLO, 0, 512, 2), (RLO, 512, 512, 3)], post3, need_rt=True)
    barrier()

    load_slab(RLO, wz2_d, 0, 1024)
    op(ACT, lambda e: e.activation(out=ESINK[:], in_=ESINK[:], func=ACTF.Exp), writes=[t_const])

    KBZ = [XS[0][:].bitcast(BF16).rearrange("p (k t) -> p k t", k=2), XS[1][:].bitcast(BF16).rearrange("p (k t) -> p k t", k=2)]
    op(DVE, lambda e: e.memset(KBZ[0][64:128, :, :], 0.0), writes=[t_xs[0]])
    op(DVE, lambda e: e.memset(KBZ[1][0:64, :, :], 0.0), writes=[t_xs[1]])
    op(DVE, lambda e: e.tensor_copy(out=KBZ[0][0:64, :, :], in_=KT_B[0:64, :, :]), reads=[t_kv], writes=[t_xs[0]])
    op(ACT, lambda e: e.copy(out=KBZ[1][64:128, :, :], in_=KT_B[64:128, :, :]), reads=[t_kv], writes=[t_xs[1]])

    def b_emit_s(u, st, gstep, xi):
        kt, lo, hi, mk = st
        w = hi - lo
        sbs = [(0, 1), (2, 7)][gstep % 2]
        px = PX[xi]
        pxv = px[:, 0:512].rearrange("p (e c) -> p e c", e=2)[:, :, 0:w]
        fns = []
        for e_ in range(2):
            fns.append(lambda e, e_=e_: e.matmul(BK[sbs[e_]][:, 0:w], lhsT=KBZ[e_][:, u.kvb, kt * 128:(kt + 1) * 128],
                                                 rhs=R2[:, 8 + u.i, u.Q * 512 + lo:u.Q * 512 + hi],
                                                 start=True, stop=not B_PE_BIAS))
        if B_PE_BIAS:
            for e_ in range(2):
                fns.append(lambda e, e_=e_: e.matmul(BK[sbs[e_]][:, 0:w], lhsT=IDENT[:], rhs=mk, start=False, stop=True))
        grp(PE, fns, reads=[u.tq, t_xs[0], t_xs[1], t_const], writes=[t_bk[sbs[0]], t_bk[sbs[1]]])
        for e_ in range(2):
            pmS = BK[sbs[e_]]
            op(ACT, lambda e: e.activation(out=px[:, e_ * 256:e_ * 256 + w], in_=pmS[:, 0:w],
                                           func=ACTF.Exp, scale=0.125),
               reads=[t_bk[sbs[e_]]], writes=[t_px[xi]])
        if not B_PE_BIAS:
            pxv = px[:, 0:512].rearrange("p (e c) -> p e c", e=2)[:, :, 0:w]
            op(DVE, lambda e: e.tensor_tensor(out=pxv, in0=pxv, in1=bc_mid(u.mk01(st), 2), op=ALU.mult),
               reads=[t_const], writes=[t_px[xi]])

    def b_emit_pv(u, st, xi):
        kt, lo, hi, mk = st
        px = PX[xi]
        fns = []
        for e_ in range(2):
            bank = BK[PVB[u.pair][e_]]
            for j in range((hi - lo) // 128):
                i = lo // 128 + j
                stf = u.first[e_]
                u.first[e_] = False
                fns.append(lambda e, e_=e_, j=j, i=i, stf=stf, bank=bank: e.matmul(
                    bank[:, i * 128:i * 128 + 65], lhsT=px[:, e_ * 256 + j * 128:e_ * 256 + (j + 1) * 128],
                    rhs=VB[:, kt, u.kvb, 0:65], start=stf, stop=False, skip_group_check=True))
        grp(PE, fns, reads=[t_px[xi], t_kv], writes=[t_bk[PVB[u.pair][0]], t_bk[PVB[u.pair][1]]])

    def b_finish(u):
        pr = u.pair
        tb = [t_bk[PVB[pr][0]], t_bk[PVB[pr][1]]]
        pvb = PVT[pr][:].rearrange("p (e t c) -> p e t c", e=2, t=4)
        op(DVE, lambda e: e.tensor_tensor(out=RD[:].rearrange("p (e t) -> p e t", e=2), in0=pvb[:, :, :, 64],
                                          in1=bc_last(ESINK[:, 2 * u.i:2 * u.i + 2], 4), op=ALU.add),
           reads=tb + [t_const], writes=[t_rd])
        op(DVE, lambda e: e.reciprocal(out=RD2[:], in_=RD[:]), writes=[t_rd])
        qc = slice(u.Q * 512, (u.Q + 1) * 512)
        op(DVE, lambda e: e.tensor_tensor(out=QY[:, 8 + u.i, qc].rearrange("p (t e d) -> p e t d", t=4, e=2),
                                          in0=pvb[:, :, :, 0:64],
                                          in1=bc_last(RD2[:].rearrange("p (e t) -> p e t", e=2), 64), op=ALU.mult),
           reads=tb + [t_rd], writes=[u.tq])

    bunits = []
    for i in range(8):
        for Q in range(NQG):
            u = Unit()
            u.i, u.Q, u.kvb = i, Q, i // 4
            u.tq = t_q[8 + i][Q]
            steps = []
            for r in range(-1, 4):
                kt = 4 * Q + r
                if kt < 0:
                    continue
                if r < 0:
                    steps.append((kt, 0, 128, TRAT[:, 128:256]))
                elif r < 3:
                    steps.append((kt, r * 128, (r + 2) * 128, TRAT[:]))
                else:
                    steps.append((kt, 384, 512, TRAT[:, 0:128]))
            u.steps = steps
            u.mk01 = lambda st: (ATRI[:] if st[2] - st[1] == 128 and st[1] == 0 else (TRI[:] if st[2] - st[1] == 128 else TRAT01[:]))
            u.emit_s, u.emit_pv, u.finish = b_emit_s, b_emit_pv, b_finish
            bunits.append(u)
    run_units(bunits)
    barrier()

    if "ob" in debug:
        dd = dout("dbg_OB", [128, 8, S], BF16)
        dma(SP, d_out[0], dd, QY[:, 8:16, :])
        barrier()

    load_slab(RHI, wout_d, 1024, 1024, blk0=2)
    proj_pass([(RLO, 0, 512, 0), (RLO, 512, 512, 1)], post_z(8))
    barrier()

    load_slab(RLO, wout_d, 0, 1024)
    FG = flat(KT_B[:]).bitcast(F32)
    dma(SP, d_const, FG, fg_d, writes=[t_const])
    xv = x_d.rearrange("(t p) d -> t p d", p=128)
    ov = out_d.rearrange("(t p) d -> t p d", p=128)
    def epilogue(tt, b):
        def run():
            op(ACT, lambda e: e.activation(out=HB[:], in_=XS[b][:], func=ACTF.Square, accum_out=SS2[:, tt:tt + 1]),
               reads=[t_xs[b]], writes=[t_hb, t_rstd])
            op(ACT, lambda e: e.activation(out=SS2[:, tt:tt + 1], in_=SS2[:, tt:tt + 1], func=ACTF.Sqrt,
                                           scale=1.0 / D, bias=EPS_T[:, 0:1]), writes=[t_rstd])
            op(DVE, lambda e: e.reciprocal(out=RS2[:, tt:tt + 1], in_=SS2[:, tt:tt + 1]), writes=[t_rstd])
            op(DVE, lambda e: e.scalar_tensor_tensor(out=XS[b][:], in0=XS[b][:], scalar=RS2[:, tt:tt + 1], in1=FG,
                                                     op0=ALU.mult, op1=ALU.mult),
               reads=[t_rstd, t_const], writes=[t_xs[b]])
            dma(SP, d_out[b], ov[tt], XS[b][:], reads=[t_xs[b]])
        return run

    p5_cnt = 0
    sched = [(0, 2), (0, 3), (1, 2), (1, 3), (0, 0), (0, 1), (1, 0), (1, 1)]
    for tt in range(2, NT):
        sched += [(tt, 2), (tt, 3), (tt, 0), (tt, 1)]
    done_blocks = {}
    loaded = set()
    epi_q = []
    for (tt, blk) in sched:
        b = tt % 2
        ts_ = slice(tt * 128, (tt + 1) * 128)
        if tt not in loaded:
            loaded.add(tt)
            dma(SP, d_xs[b], XS[b][:], xv[tt], writes=[t_xs[b]])
        pi = p5_cnt % 8
        p5_cnt += 1
        grp(PE, [(lambda e, c=c: e.matmul(BK[pi][:, 0:512], lhsT=QY[:, c, ts_],
                                          rhs=(RLO if blk < 2 else RHI)[:, c, (blk % 2) * 512:(blk % 2 + 1) * 512],
                                          start=(c == 0), stop=(c == 15))) for c in range(16)],
            reads=[t_slabb[blk]], writes=[t_bk[pi]])
        op(DVE, lambda e: e.tensor_tensor(out=XS[b][:, blk * 512:(blk + 1) * 512], in0=BK[pi][:, 0:512],
                                          in1=XS[b][:, blk * 512:(blk + 1) * 512], op=ALU.add),
           reads=[t_bk[pi]], writes=[t_xs[b]])
        done_blocks[tt] = done_blocks.get(tt, 0) + 1
        for item in epi_q:
            item[1] -= 1
        while epi_q and epi_q[0][1] <= 0:
            epi_q.pop(0)[0]()
        if done_blocks[tt] == 4:
            epi_q.append([epilogue(tt, b), 2])
    for item in epi_q:
        item[0]()

    for d in (d_out[0], d_out[1]):
        if d.n > 0:
            SP.wait(Ev(d, d.n))
    return nc, ctx


def make_in_maps(x, w_in, cmp_k_w1, cmp_k_w2, cmp_v_w1, cmp_v_w2, cmp_k_pos, cmp_v_pos,
                 sinks, w_out, norm_g, final_g, n_cores=8):
    c = _consts()
    w_kv, w_q1, w_q2, w_z2 = _split_w_in(np.asarray(w_in[0], np.float32))
    shared = {
        "w_kv": w_kv, "w_q1": w_q1, "w_q2": w_q2, "w_z2": w_z2,
        "w_out": np.ascontiguousarray(w_out[0], np.float32),
        "cmp_k_w1": np.ascontiguousarray(cmp_k_w1[0], np.float32),
        "cmp_k_w2": np.ascontiguousarray(cmp_k_w2[0], np.float32),
        "cmp_v_w1": np.ascontiguousarray(cmp_v_w1[0], np.float32),
        "cmp_v_w2": np.ascontiguousarray(cmp_v_w2[0], np.float32),
        "kposT": np.ascontiguousarray(np.asarray(cmp_k_pos[0], np.float32).T),
        "vposT": np.ascontiguousarray(np.asarray(cmp_v_pos[0], np.float32).T),
        "sinks_bc": np.ascontiguousarray(np.broadcast_to(np.asarray(sinks[0], np.float32)[None, :], (128, 16))),
        "g_col": np.ascontiguousarray(np.asarray(norm_g[0], np.float32).reshape(16, 128).T),
        "fg_bc": np.ascontiguousarray(np.broadcast_to(np.asarray(final_g, np.float32)[None, :], (128, D))),
    }
    shared.update(c)
    in_maps = []
    for b in range(n_cores):
        m = dict(shared)
        m["x"] = np.ascontiguousarray(x[b], np.float32)
        in_maps.append(m)
    return in_maps


def kernel(**inputs):
    inputs = {k: np.asarray(v) for k, v in inputs.items()}
    in_maps = make_in_maps(**inputs)
    nc, ctx = build()
    with ctx:
        res = run_bass_kernel_spmd(nc, in_maps, core_ids=list(range(8)))
    outs = [np.asarray(r["out"], np.float32) for r in res.results]
    return np.stack(outs, axis=0)
```
